# Optimizing a Trainium2 kernel written in Bass

```python
import jax, jax.numpy as jnp
from jax import lax
import numpy as np

D_MODEL = 1024
BATCH = 4
SEQ = 8192
DEPTH = 2

CHUNK = 64
MEM_LEN = 256
EPS = 1e-6
NEG_INF = -1e30

A_Q_HEADS = 8
A_KV_HEADS = 2
A_HEAD_DIM = 64
A_WINDOW = 128
A_WIN_CHUNKS = A_WINDOW // CHUNK
A_WIDTH = A_Q_HEADS * A_HEAD_DIM

B_HEADS = 4
B_KEY_DIM = 128
B_VAL_DIM = 128
B_WIDTH = B_HEADS * B_VAL_DIM
ROT_BASE = 10000.0

C_WIDTH = D_MODEL
C_CONV = 3

X_HEADS = 4
X_HEAD_DIM = D_MODEL // X_HEADS

EVEN_SPLITS = (A_WIDTH,
               A_KV_HEADS * A_HEAD_DIM,
               A_KV_HEADS * A_HEAD_DIM,
               A_WIDTH,
               B_HEADS * B_KEY_DIM,
               B_HEADS * B_KEY_DIM,
               B_WIDTH,
               B_WIDTH)
EVEN_IN = sum(EVEN_SPLITS)
EVEN_SPLIT_POINTS = tuple(int(p) for p in np.cumsum(EVEN_SPLITS)[:-1])
MIX_WIDTH = A_WIDTH + B_WIDTH
ODD_IN = 4 * C_WIDTH
N_EVEN = (DEPTH + 1) // 2
N_ODD = DEPTH // 2

kernel_name = "hybrid_swa_retention_shortconv_memxattn"


def rmsnorm(x, g):
    xf = x.astype(jnp.float32)
    y = xf * lax.rsqrt(jnp.mean(xf * xf, axis=-1, keepdims=True) + EPS)
    return (y * g.astype(jnp.float32)).astype(x.dtype)


def swa_sink_attention(q, k, v, sink):
    b, s, _, d = q.shape
    n = s // CHUNK
    g = A_Q_HEADS // A_KV_HEADS
    nband = (A_WIN_CHUNKS + 1) * CHUNK
    qc = q.reshape(b, n, CHUNK, A_KV_HEADS, g, d)

    def band(t):
        tc = t.reshape(b, n, CHUNK, A_KV_HEADS, d)
        tp = jnp.pad(tc, ((0, 0), (A_WIN_CHUNKS, 0), (0, 0), (0, 0), (0, 0)))
        return jnp.concatenate([tp[:, j:j + n] for j in range(A_WIN_CHUNKS + 1)], axis=2)

    kb, vb = band(k), band(v)
    key_chunk = jnp.arange(n)[:, None] - A_WIN_CHUNKS + jnp.arange(nband)[None, :] // CHUNK
    valid = key_chunk >= 0
    scores = jnp.einsum('bnqhgd,bnkhd->bnhgqk', qc, kb).astype(jnp.float32) * (d ** -0.5)
    scores = jnp.where(valid[None, :, None, None, None, :], scores, NEG_INF)
    sink_logit = sink.astype(jnp.float32).reshape(1, 1, A_KV_HEADS, g, 1, 1)
    m = jnp.maximum(jnp.max(scores, axis=-1, keepdims=True), sink_logit)
    p = jnp.exp(scores - m)
    denom = jnp.sum(p, axis=-1, keepdims=True) + jnp.exp(sink_logit - m)
    p = (p / denom).astype(v.dtype)
    out = jnp.einsum('bnhgqk,bnkhd->bnqhgd', p, vb)
    return out.reshape(b, s, A_Q_HEADS * d)


def rotary(t, pos):
    d = t.shape[-1]
    inv = 1.0 / (ROT_BASE ** jnp.linspace(0.0, 1.0, d // 2, dtype=jnp.float32))
    ang = pos.astype(jnp.float32)[:, None] * inv[None, :]
    cos = jnp.cos(ang)[None, :, None, :].astype(t.dtype)
    sin = jnp.sin(ang)[None, :, None, :].astype(t.dtype)
    t1, t2 = t[..., :d // 2], t[..., d // 2:]
    return jnp.concatenate([t1 * cos - t2 * sin, t1 * sin + t2 * cos], axis=-1)


def retention(q, k, v):
    b, s, h, dk = q.shape
    dv = v.shape[-1]
    n = s // CHUNK
    dt = q.dtype
    pos = jnp.arange(s)
    q = rotary(q, pos)
    k = rotary(k, pos) * (dk ** -0.5)
    log_gamma = jnp.log1p(-jnp.exp2(-5.0 - jnp.arange(h, dtype=jnp.float32)))
    idx = jnp.arange(CHUNK, dtype=jnp.float32)
    intra = jnp.exp(log_gamma[:, None, None] * jnp.abs(idx[:, None] - idx[None, :]))
    k_decay = jnp.exp(log_gamma[:, None] * (CHUNK - 1 - idx)[None, :])
    q_decay = jnp.exp(log_gamma[:, None] * (idx + 1.0)[None, :])
    chunk_decay = jnp.exp(log_gamma * CHUNK).astype(dt)[None, :, None, None]

    qc = q.reshape(b, n, CHUNK, h, dk)
    kc = k.reshape(b, n, CHUNK, h, dk)
    vc = v.reshape(b, n, CHUNK, h, dv)
    sc = jnp.einsum('bnihd,bnjhd->bnhij', qc, kc) * intra.astype(dt)[None, None]
    o_intra = jnp.einsum('bnhij,bnjhe->bnihe', sc, vc)
    kv = jnp.einsum('bnjhd,bnjhe,hj->nbhde', kc, vc, k_decay.astype(dt))

    def step(state, kv_c):
        return state * chunk_decay + kv_c, state

    _, states = lax.scan(step, jnp.zeros((b, h, dk, dv), dt), kv)
    o_cross = jnp.einsum('bnihd,nbhde,hi->bnihe', qc, states, q_decay.astype(dt))
    o = (o_intra + o_cross).astype(jnp.float32)
    o = o * lax.rsqrt(jnp.mean(o * o, axis=-1, keepdims=True) + EPS)
    return o.astype(dt).reshape(b, s, h * dv)


def even_mixer(xn, w_in, sink, w_out):
    b, s, _ = xn.shape
    hcat = xn @ w_in
    aq, ak, av, ag, bq, bk, bv, bg = jnp.split(hcat, EVEN_SPLIT_POINTS, axis=-1)
    ya = swa_sink_attention(aq.reshape(b, s, A_Q_HEADS, A_HEAD_DIM),
                            ak.reshape(b, s, A_KV_HEADS, A_HEAD_DIM),
                            av.reshape(b, s, A_KV_HEADS, A_HEAD_DIM), sink)
    yb = retention(bq.reshape(b, s, B_HEADS, B_KEY_DIM),
                   bk.reshape(b, s, B_HEADS, B_KEY_DIM),
                   bv.reshape(b, s, B_HEADS, B_VAL_DIM))
    y = jnp.concatenate([ya * jax.nn.silu(ag), yb * jax.nn.silu(bg)], axis=-1)
    return y @ w_out


def conv_mixer(xn, w_in, conv_w, conv_b, w_out):
    s = xn.shape[1]
    gb, gc, u, gate = jnp.split(xn @ w_in, 4, axis=-1)
    z = gc * u
    zp = jnp.pad(z, ((0, 0), (C_CONV - 1, 0), (0, 0)))
    conv = conv_b
    for j in range(C_CONV):
        conv = conv + zp[:, j:j + s] * conv_w[j]
    y = gb * conv * jax.nn.silu(gate)
    return y @ w_out


def memory_cross_attention(xn, memn, wq, wkv, wo):
    b, s, _ = xn.shape
    q = (xn @ wq).reshape(b, s, X_HEADS, X_HEAD_DIM)
    k, v = jnp.split(memn @ wkv, 2, axis=-1)
    k = k.reshape(b, -1, X_HEADS, X_HEAD_DIM)
    v = v.reshape(b, -1, X_HEADS, X_HEAD_DIM)
    sc = jnp.einsum('bshd,bmhd->bhsm', q, k).astype(jnp.float32) * (X_HEAD_DIM ** -0.5)
    p = jax.nn.softmax(sc, axis=-1).astype(v.dtype)
    o = jnp.einsum('bhsm,bmhd->bshd', p, v).reshape(b, s, D_MODEL)
    return o @ wo


def setup_inputs(seed: int = 0) -> dict:
    key = jax.random.key(seed)
    ks = jax.random.split(key, 20)
    f32 = jnp.float32

    def w(k, shape, fan_in):
        return jax.random.normal(k, shape, f32) * (fan_in ** -0.5)

    def gain(k, shape):
        return 1.0 + 0.01 * jax.random.normal(k, shape, f32)

    return {
        "x": jax.random.normal(ks[0], (BATCH, SEQ, D_MODEL), f32),
        "mem": jax.random.normal(ks[1], (BATCH, MEM_LEN, D_MODEL), f32),
        "e_norm": gain(ks[2], (N_EVEN, D_MODEL)),
        "e_w_in": w(ks[3], (N_EVEN, D_MODEL, EVEN_IN), D_MODEL),
        "e_sink": 0.5 * jax.random.normal(ks[4], (N_EVEN, A_Q_HEADS), f32),
        "e_w_out": w(ks[5], (N_EVEN, MIX_WIDTH, D_MODEL), MIX_WIDTH),
        "o_norm": gain(ks[6], (N_ODD, D_MODEL)),
        "o_w_in": w(ks[7], (N_ODD, D_MODEL, ODD_IN), D_MODEL),
        "o_conv_w": w(ks[8], (N_ODD, C_CONV, C_WIDTH), C_CONV),
        "o_conv_b": 0.01 * jax.random.normal(ks[9], (N_ODD, C_WIDTH), f32),
        "o_w_out": w(ks[10], (N_ODD, C_WIDTH, D_MODEL), C_WIDTH),
        "c_norm": gain(ks[11], (DEPTH, D_MODEL)),
        "c_mem_norm": gain(ks[12], (DEPTH, D_MODEL)),
        "c_wq": w(ks[13], (DEPTH, D_MODEL, D_MODEL), D_MODEL),
        "c_wkv": w(ks[14], (DEPTH, D_MODEL, 2 * D_MODEL), D_MODEL),
        "c_wo": w(ks[15], (DEPTH, D_MODEL, D_MODEL), D_MODEL),
        "final_norm": gain(ks[16], (D_MODEL,)),
    }


def reference(x, mem, e_norm, e_w_in, e_sink, e_w_out, o_norm, o_w_in, o_conv_w, o_conv_b,
              o_w_out, c_norm, c_mem_norm, c_wq, c_wkv, c_wo, final_norm):
    for i in range(DEPTH):
        j = i // 2
        if i % 2 == 0:
            x = x + even_mixer(rmsnorm(x, e_norm[j]), e_w_in[j], e_sink[j], e_w_out[j])
        else:
            x = x + conv_mixer(rmsnorm(x, o_norm[j]), o_w_in[j], o_conv_w[j], o_conv_b[j], o_w_out[j])
        x = x + memory_cross_attention(rmsnorm(x, c_norm[i]), rmsnorm(mem, c_mem_norm[i]),
                                       c_wq[i], c_wkv[i], c_wo[i])
    return rmsnorm(x, final_norm)
```

```python
import numpy as np
import concourse.bass as bass
import concourse.mybir as mybir
from concourse.bass_utils import run_bass_kernel_spmd

F32 = mybir.dt.float32
BF16 = mybir.dt.bfloat16
AF = mybir.ActivationFunctionType
ALU = mybir.AluOpType

NT_MAIN = 33
NT_PRE = 31
D = 1024
EPS = 1e-6
GROUPS = [(0, 1)] + [(1 + 4 * i, 4) for i in range(8)]
NSLOT = 6


class T:
    __slots__ = ("name", "writes", "reads", "sem", "cnt", "alias", "excl", "last_read")

    def __init__(self, name, excl=False):
        self.name = name
        self.excl = excl
        self.writes = []
        self.reads = []
        self.sem = None
        self.cnt = 0
        self.alias = []
        self.last_read = 0


class Op:
    __slots__ = ("eng", "sem", "val")

    def __init__(self, eng, sem, val):
        self.eng = eng
        self.sem = sem
        self.val = val


def _compress(lst):
    last = {}
    for o in lst:
        k = id(o.sem)
        if k not in last or last[k].val < o.val:
            last[k] = o
    return list(last.values())


class Sched:
    COMPUTE = ("pe", "act", "dve", "pool")

    def __init__(self, nc):
        self.nc = nc
        self.E = {"pe": nc.tensor, "act": nc.scalar, "dve": nc.vector,
                  "pool": nc.gpsimd, "sp": nc.sync}
        self.sem = {e: nc.alloc_semaphore("s_" + e) for e in self.COMPUTE}
        self.cnt = {e: 0 for e in self.COMPUTE}
        self.waited = {e: {} for e in self.E}
        self.nops = {e: 0 for e in self.E}
        self.nwaits = 0
        self.opidx = 0
        self.label = 'init'
        self.pe_labels = []

    def _wait(self, eng, d):
        w = self.waited[eng]
        key = id(d.sem)
        if w.get(key, 0) >= d.val:
            return
        w[key] = d.val
        self.E[eng].wait_ge(d.sem, d.val)
        self.nwaits += 1

    def _deps(self, eng, reads, writes, is_dma):
        deps = []
        for t in reads:
            for d in t.writes:
                if (not is_dma) and d.eng == eng and eng == "pe":
                    continue
                deps.append(d)
        for t in writes:
            for tt in [t] + t.alias:
                for d in tt.reads:
                    if (not is_dma) and d.eng == eng:
                        continue
                    deps.append(d)
                for d in tt.writes:
                    if (not is_dma) and d.eng == eng:
                        continue
                    deps.append(d)
        return deps

    def _record(self, op, reads, writes):
        self.opidx += 1
        for t in reads:
            t.last_read = self.opidx
            t.reads.append(op)
            if len(t.reads) > 48:
                t.reads = _compress(t.reads)
        for t in writes:
            for a in t.alias:
                a.reads = []
                a.writes = []
            if t.reads:
                t.writes = [op]
                t.reads = []
            else:
                t.writes.append(op)
                if len(t.writes) > 48:
                    t.writes = _compress(t.writes)

    def op(self, eng, fn, reads=(), writes=(), signal=True):
        ex = [t for t in reads if t.excl]
        if ex:
            reads = [t for t in reads if not t.excl]
            writes = list(writes) + ex
        for d in self._deps(eng, reads, writes, False):
            self._wait(eng, d)
        ins = fn(self.E[eng])
        self.nops[eng] += 1
        if eng == 'pe':
            self.pe_labels.append(self.label)
        if signal:
            self.cnt[eng] += 1
            ins.then_inc(self.sem[eng], 1)
            op = Op(eng, self.sem[eng], self.cnt[eng])
        else:
            op = Op(eng, self.sem[eng], self.cnt[eng] + 1)
        self._record(op, reads, writes)
        return op

    def dma(self, q, out_ap, in_ap, reads=(), writes=(), semt=None, **kw):
        for d in self._deps(q, reads, writes, True):
            self._wait(q, d)
        if semt is None:
            semt = writes[0] if writes else reads[0]
        if semt.sem is None:
            semt.sem = self.nc.alloc_semaphore("d_" + semt.name)
        ins = self.E[q].dma_start(out=out_ap, in_=in_ap, **kw)
        semt.cnt += 16
        ins.then_inc(semt.sem, 16)
        op = Op("dma", semt.sem, semt.cnt)
        self.nops[q] += 1
        self._record(op, reads, writes)
        return op


class Buf:
    def __init__(self, t, tiles):
        self.t = t
        self.T = tiles


def _blocks():
    B = []
    perm = []
    for fc in range(4):
        perm += [(fc * 64, 64), ((4 + fc) * 64, 64)]
    B.append(("e_w_in", 0, "e_norm", [(c0, n) for (c0, n) in perm]))
    B.append(("e_w_in", 0, "e_norm", [(768 + c0, n) for (c0, n) in perm]))
    B.append(("e_w_in", 0, "e_norm", [(2816, 512)]))
    B.append(("e_w_in", 0, "e_norm", [(512, 256), (512, 256)]))
    B.append(("e_w_in", 0, "e_norm", [(1280, 512)]))
    B.append(("e_w_in", 0, "e_norm", [(1792, 512)]))
    B.append(("e_w_in", 0, "e_norm", [(2304, 512)]))
    B.append(("e_w_out", 0, "PERM", [(0, 512)]))
    B.append(("e_w_out", 0, "PERM", [(512, 512)]))
    B.append(("c_wq", 0, "c_norm0", [(0, 512)]))
    B.append(("c_wq", 0, "c_norm0", [(512, 512)]))
    B.append(("c_wo", 0, None, [(0, 512)]))
    B.append(("c_wo", 0, None, [(512, 512)]))
    for j in range(4):
        B.append(("o_w_in", 0, "o_norm", [(1024 + 256 * j, 256), (2048 + 256 * j, 256)]))
        B.append(("o_w_in", 0, "o_norm", [(256 * j, 256), (3072 + 256 * j, 256)]))
    B.append(("o_w_out", 0, None, [(0, 512)]))
    B.append(("o_w_out", 0, None, [(512, 512)]))
    B.append(("c_wq", 1, "c_norm1", [(0, 512)]))
    B.append(("c_wq", 1, "c_norm1", [(512, 512)]))
    B.append(("c_wo", 1, None, [(0, 512)]))
    B.append(("c_wo", 1, None, [(512, 512)]))
    for l in range(2):
        for j in range(4):
            B.append(("c_wkv", l, "c_mem_norm%d" % l, [(512 * j, 512)]))
    return B


BLOCKS = _blocks()
NBLK_STREAM = 27
GAMMA = [1.0 - 2.0 ** (-5.0 - h) for h in range(4)]


def build(phases=(0, 1, 2, 3), final=True, groups=None, stage=9, nprep=None, dbg=None):
    dbg = dbg or {}
    nc = bass.Bass("TRN2", target_bir_lowering=False)
    S = Sched(nc)
    groups = GROUPS if groups is None else groups

    def din(name, shape, dt=F32):
        return nc.dram_tensor(name, list(shape), dt, kind="ExternalInput")

    xm = din("xm", [NT_MAIN * 128, D])
    xp = din("xp", [NT_PRE * 128, D])
    memd = din("mem", [256, D])
    W = {
        "e_w_in": [din("e_w_in", [D, 3328])],
        "e_w_out": [din("e_w_out", [D, D])],
        "o_w_in": [din("o_w_in", [D, 4096])],
        "o_w_out": [din("o_w_out", [D, D])],
        "c_wq": [din("c_wq0", [D, D]), din("c_wq1", [D, D])],
        "c_wkv": [din("c_wkv0", [D, 2048]), din("c_wkv1", [D, 2048])],
        "c_wo": [din("c_wo0", [D, D]), din("c_wo1", [D, D])],
    }
    gains_d = din("gains", [128, 6, 8])
    gfin_d = din("gfin", [128, D])
    sink_d = din("sink", [128, 4])
    cw_d = din("cw", [128, 4, 8])
    csm_d = din("csm", [128, NT_MAIN, 192])
    csp_d = din("csp", [128, NT_PRE, 192])
    dtab_d = din("dtab", [128, 512])
    qdtab_d = din("qdtab", [128, 512])
    kdtab_d = din("kdtab", [128, 4])
    kdp_d = din("kdp", [128, NT_PRE, 4])
    hbias_d = din("hbias", [128, 1])
    ident_d = din("ident", [128, 128])
    out_d = nc.dram_tensor("out", [NT_MAIN * 128, D], F32, kind="ExternalOutput")
    wsc = nc.dram_tensor("wsc", [len(BLOCKS), 128, 4096], BF16)
    GIDX = {"e_norm": 0, "c_norm0": 1, "o_norm": 2, "c_norm1": 3, "c_mem_norm0": 4, "c_mem_norm1": 5}

    def sb(name, shape, dt, ntiles=1):
        t = nc.alloc_sbuf_tensor("sb_" + name, list(shape), dt)
        return Buf(t, [T(name + str(i)) for i in range(ntiles)])

    xbuf = sb("xbuf", [128, 4, D], F32, 4)
    obuf = sb("obuf", [128, 2, D], F32, 2)
    ring = sb("ring", [128, NSLOT, 8, 512], BF16, NSLOT)
    xnT = sb("xnT", [128, 8, 512], BF16, 4)
    xs = sb("xs", [128, 2, D], BF16, 2)
    stat = sb("stat", [128, 24], F32, 6)
    fmA = sb("fmA", [128, 8, 512], BF16, 2)
    fmY = sb("fmY", [128, 8, 512], BF16, 8)
    sbg = sb("sbg", [128, 4, 512], BF16)
    akT = sb("akT", [128, 640], BF16)
    vbuf = sb("vbuf", [128, 5, 2, 128], BF16, 5)
    rtA = sb("rtA", [128, D], F32)
    rtB = sb("rtB", [128, D], F32)
    qkr = sb("qkr", [128, 2, D], BF16, 2)
    kd = sb("kd", [128, 2, 512], BF16, 2)
    vtok = sb("vtok", [128, 2, 512], BF16, 2)
    qkT = sb("qkT", [128, 8, 128], BF16)
    qdT = sb("qdT", [128, 4, 128], BF16)
    scT = sb("scT", [128, 512], BF16)
    stf = sb("stf", [128, 512], F32)
    stb = sb("stb", [128, 512], BF16)
    ybn = sb("ybn", [128, 512], BF16)
    PT = sb("PT", [128, 8, 2, 128], BF16, 2)
    rd = sb("rd", [128, 512], F32)
    PT2 = Buf(PT.t[:].rearrange("p h b n -> p (h b n)").rearrange("p (a m n) -> p a m n", a=2, m=2), PT.T)
    KmT = sb("KmT", [128, 2, 8, 256], BF16, 2)
    Vm = sb("Vm", [128, 2, 2, D], BF16, 2)
    zbuf = sb("zbuf", [128, 8, 516], F32, 8)
    gcs = sb("gcs", [128, 2, 512], F32, 2)
    cvt = sb("cvt", [128, 2, 512], F32, 2)
    sgb = sb("sgb", [128, 2, 512], F32, 2)
    gains = sb("gains", [128, 6, 8], F32)
    gfin = sb("gfin", [128, D], F32)
    esink = sb("esink", [128, 4], F32)
    cw = sb("cw", [128, 4, 8], F32)
    cs = sb("cs", [128, 4, 192], F32, 4)
    dtab = sb("dtab", [128, 512], F32)
    qdtab = sb("qdtab", [128, 512], F32)
    kdtab = sb("kdtab", [128, 4], F32)
    kdp = sb("kdp", [128, NT_PRE, 4], F32, 1)
    hbias = sb("hbias", [128, 1], F32)
    identf = Buf(rd.t[:, 0:128], rd.T)
    ident = sb("ident", [128, 128], BF16)
    ones = sb("ones", [128, 3, 128], BF16)
    xpb = sb("xpb", [128, 3, D], F32, 3)

    numsb = xpb.t[:, 0].bitcast(BF16).rearrange("p (a d n) -> p a d n", a=2, d=2)
    rdx = xpb.t[:, 1].rearrange("p (a n) -> p a n", a=2)
    numsb_T = [T("numsb0"), T("numsb1")]
    rdx_T = [T("rdx0"), T("rdx1")]
    for tt in numsb_T:
        tt.alias = [xpb.T[0]]
    for tt in rdx_T:
        tt.alias = [xpb.T[1]]
    xpb.T[0].alias = list(numsb_T)
    xpb.T[1].alias = list(rdx_T)

    rdh = [T("rdh0"), T("rdh1")]
    for tt in rdh:
        tt.alias = [rd.T[0]]
    rd.T[0].alias = list(rdh)
    x5T = T("x5")
    x5T.alias = [xpb.T[2]]
    xpb.T[2].alias = [x5T]
    XB = [(xbuf.t[:, j], xbuf.T[j]) for j in range(4)] + [(xpb.t[:, 2], x5T)]
    xmap = {}

    def xa(i):
        return XB[xmap[i]][0]

    def xT(i):
        return XB[xmap[i]][1]

    pst = nc.alloc_psum_tensor("ps", [128, 8, 512], F32)
    PB = [T("psb%d" % i, excl=True) for i in range(8)]
    prr = [0]
    reserved = set()

    def ps1():
        while True:
            i = prr[0] % 8
            prr[0] += 1
            if i not in reserved:
                return i

    def ps2():
        while True:
            if prr[0] % 2:
                prr[0] += 1
            i = prr[0] % 8
            prr[0] += 2
            if i not in reserved and (i + 1) not in reserved:
                return i

    def psb16(i):
        return pst[:, i, :].bitcast(BF16)

    def bcast(buf, off, dims):
        t = buf.t
        prow = int(np.prod(t.shape[1:]))
        return bass.AP(t, off, [[prow, 128]] + [[s, c] for (s, c) in dims])

    cq = "pool"
    S.dma(cq, gains.t[:], gains_d.ap(), writes=gains.T)
    S.dma(cq, gfin.t[:], gfin_d.ap(), writes=gfin.T)
    S.dma(cq, esink.t[:], sink_d.ap(), writes=esink.T)
    S.dma(cq, cw.t[:], cw_d.ap(), writes=cw.T)
    S.dma(cq, dtab.t[:], dtab_d.ap(), writes=dtab.T)
    S.dma(cq, qdtab.t[:], qdtab_d.ap(), writes=qdtab.T)
    S.dma(cq, kdtab.t[:], kdtab_d.ap(), writes=kdtab.T)
    S.dma(cq, hbias.t[:], hbias_d.ap(), writes=hbias.T)
    S.dma(cq, kdp.t[:], kdp_d.ap(), writes=kdp.T)
    S.dma(cq, identf.t, ident_d.ap(), writes=identf.T)
    S.op("dve", lambda e: e.tensor_copy(ident.t[:], identf.t), reads=identf.T, writes=ident.T)
    S.op("act", lambda e: e.activation(esink.t[:], esink.t[:], AF.Exp), reads=esink.T, writes=esink.T)
    S.op("pool", lambda e: e.memset(ones.t[:], 0.0), writes=ones.T)
    S.op("pool", lambda e: e.memset(ones.t[:, 0, 0:64], 1.0), writes=ones.T)
    S.op("pool", lambda e: e.memset(ones.t[:, 1, 64:128], 1.0), writes=ones.T)
    S.op("pool", lambda e: e.memset(ones.t[:, 2, :], 1.0), writes=ones.T)
    S.op("pool", lambda e: e.memset(vbuf.t[:], 1.0), writes=vbuf.T)
    S.op("dve", lambda e: e.memset(akT.t[:], 0.0), writes=akT.T)

    wsc_T = [T("wsc%d" % b) for b in range(len(BLOCKS))]
    STG = [(xbuf.t[:, 0:4].rearrange("p a (kc n) -> p (a kc) n", kc=2), xbuf.T),
           (zbuf.t[:, :, 0:512], zbuf.T)]
    OST = [(obuf.t[:].rearrange("p a n -> p (a n)").bitcast(BF16), obuf.T),
           (fmA.t[:].rearrange("p c n -> p (c n)"), fmA.T)]
    prep_cnt = [0]

    def prep_block(b):
        k = prep_cnt[0]
        prep_cnt[0] += 1
        wname, l, gname, segs = BLOCKS[b]
        wd = W[wname][l]
        stt, Ts = STG[k % 2]
        ob, To = OST[k % 2]
        Ts = list(Ts)
        To = list(To)
        c = 0
        for (c0, n) in segs:
            if gname == "PERM":
                for kc in range(4):
                    for hf in range(2):
                        r0 = (kc + 4 * hf) * 64
                        S.dma("sp", stt[hf * 64:(hf + 1) * 64, kc, c:c + n],
                              wd.ap()[r0:r0 + 64, c0:c0 + n], writes=Ts, semt=Ts[0])
                src = wd.ap()[512:1024, c0:c0 + n].rearrange("(kc p) n -> p kc n", p=128)
                S.dma("sp", stt[:, 4:8, c:c + n], src, writes=Ts, semt=Ts[0])
            else:
                src = wd.ap()[:, c0:c0 + n].rearrange("(kc p) n -> p kc n", p=128)
                S.dma("sp", stt[:, :, c:c + n], src, writes=Ts, semt=Ts[0])
            c += n
        ob4 = ob.rearrange("p (k n) -> p k n", k=8)
        for kc in range(8):
            src = stt[:, kc, :]
            if gname in GIDX:
                gi = GIDX[gname]
                if kc % 2 == 0:
                    S.op("dve", lambda e, kc=kc, src=src: e.tensor_scalar_mul(ob4[:, kc, :], src, gains.t[:, gi, kc:kc + 1]),
                         reads=Ts + gains.T, writes=To)
                else:
                    S.op("act", lambda e, kc=kc, src=src: e.mul(ob4[:, kc, :], src, gains.t[:, gi, kc:kc + 1]),
                         reads=Ts + gains.T, writes=To)
            else:
                if kc % 2 == 0:
                    S.op("dve", lambda e, kc=kc, src=src: e.tensor_copy(ob4[:, kc, :], src), reads=Ts, writes=To)
                else:
                    S.op("act", lambda e, kc=kc, src=src: e.copy(ob4[:, kc, :], src), reads=Ts, writes=To)
        S.dma("act", wsc.ap()[b], ob, reads=To, writes=[wsc_T[b]], semt=To[0])

    prep_rest = list(range(27, 35)) + [b for b in range(27) if b not in (5, 6, 3)]
    for b in (5, 6, 3):
        prep_block(b)

    ring_pos = [0]

    pinned = set()

    def load_block(b):
        cands = [q for q in range(NSLOT) if q not in pinned]
        assert cands, "weight ring: all slots pinned"
        s = min(cands, key=lambda q: ring.T[q].last_read)
        pinned.add(s)
        S.dma("sp", ring.t[:, s].rearrange("p k n -> p (k n)"), wsc.ap()[b], reads=[wsc_T[b]],
              writes=[ring.T[s]])
        ring.T[s].last_read = S.opidx
        return s

    def release(*slots):
        for q in slots:
            pinned.discard(q)

    def wst(s, kc, c0, n=128):
        return ring.t[:, s, kc, c0:c0 + n]

    def wmv(s, kc):
        return ring.t[:, s, kc, :]

    def norm_a(src_ap, srcT, sidx):
        st = stat.t
        Tst = [stat.T[sidx % 4]]
        c = (sidx % 4) * 4
        k = sidx % 2
        S.op("act", lambda e: e.activation(xs.t[:, k], src_ap, AF.Square, accum_out=st[:, c:c + 1]),
             reads=srcT, writes=[xs.T[k]] + Tst)
        S.op("act", lambda e: e.activation(st[:, c + 1:c + 2], st[:, c:c + 1], AF.Ln, bias=EPS, scale=1.0 / D),
             reads=Tst, writes=Tst)
        S.op("act", lambda e: e.activation(st[:, c + 2:c + 3], st[:, c + 1:c + 2], AF.Exp, scale=-0.5),
             reads=Tst, writes=Tst)
        S.op("act", lambda e: e.mul(xs.t[:, k], src_ap, st[:, c + 2:c + 3]), reads=srcT + Tst, writes=[xs.T[k]])

    def norm_b(col0, sidx):
        k = sidx % 2
        b = ps1()
        pb = psb16(b)
        for kc in range(8):
            S.op("pe", lambda e, kc=kc: e.transpose(pb[:, kc * 128:(kc + 1) * 128], xs.t[:, k, kc * 128:(kc + 1) * 128],
                                                    ident.t[:]),
                 reads=[xs.T[k]] + ident.T, writes=[PB[b]], signal=(kc == 7))
        S.op("dve", lambda e: e.tensor_copy(xnT.t[:, :, col0:col0 + 128],
                                            pb[:, :].rearrange("p (kc n) -> p kc n", kc=8)),
             reads=[PB[b]], writes=[xnT.T[col0 // 128]])

    def rmsnorm_T(src_ap, srcT, col0, sidx):
        norm_a(src_ap, srcT, sidx)
        norm_b(col0, sidx)

    def proj_fm(s, c0, N, evac):
        b = ps1()
        for kc in range(8):
            S.op("pe", lambda e, kc=kc: e.matmul(pst[:, b, 0:N], wst(s, kc, c0), xnT.t[:, kc, 0:N],
                                                 start=(kc == 0), stop=(kc == 7)),
                 reads=[ring.T[s]] + xnT.T, writes=[PB[b]], signal=(kc == 7))
        evac(pst[:, b, 0:N], [PB[b]])

    def out_proj_tile(s0, s1, i, lhs_buf, post=None):
        b = ps2()
        for cb, s in enumerate((s0, s1)):
            for kc in range(8):
                S.op("pe", lambda e, kc=kc, cb=cb, s=s: e.matmul(
                    pst[:, b + cb, :],
                    lhs_buf.t[:, kc, i * 128:(i + 1) * 128], wmv(s, kc),
                    start=(kc == 0), stop=(kc == 7)),
                    reads=[ring.T[s], lhs_buf.T[kc]], writes=[PB[b + cb]], signal=(kc == 7))
        xi, xti = xa(i), xT(i)
        S.op("dve", lambda e: e.tensor_tensor(xi.rearrange("p (a n) -> p a n", a=2),
                                              pst[:, b:b + 2, :],
                                              xi.rearrange("p (a n) -> p a n", a=2), ALU.add),
             reads=[PB[b], PB[b + 1], xti], writes=[xti])
        if post is not None:
            post(i)

    def out_proj(s0, s1, nt, lhs_buf, xtiles, post=None):
        for i in range(nt):
            out_proj_tile(s0, s1, i, lhs_buf, post)

    def rotary(src3, nh, cst, cT, srcT):
        cs_off = cst
        cosb = bass.AP(cs.t, cs_off, [[4 * 192, 128], [0, nh * 2], [1, 64]])
        sinb = bass.AP(cs.t, cs_off + 64, [[4 * 192, 128], [0, nh], [1, 64]])
        nsinb = bass.AP(cs.t, cs_off + 128, [[4 * 192, 128], [0, nh], [1, 64]])
        A3 = rtA.t[:, 0:nh * 128].rearrange("p (h d) -> p h d", h=nh)
        B3 = rtB.t[:, 0:nh * 128].rearrange("p (h d) -> p h d", h=nh)
        src2 = src3.rearrange("p h (t d) -> p (h t) d", t=2)
        A2 = rtA.t[:, 0:nh * 128].rearrange("p (h d) -> p h d", h=nh * 2)
        S.op("dve", lambda e: e.tensor_tensor(A2, src2, cosb, ALU.mult), reads=srcT + cT, writes=rtA.T)
        S.op("dve", lambda e: e.tensor_tensor(B3[:, :, 0:64], src3[:, :, 64:128], nsinb, ALU.mult),
             reads=srcT + cT, writes=rtB.T)
        S.op("dve", lambda e: e.tensor_tensor(B3[:, :, 64:128], src3[:, :, 0:64], sinb, ALU.mult),
             reads=srcT + cT, writes=rtB.T)
        S.op("pool", lambda e: e.tensor_tensor(rtA.t[:, 0:nh * 128], rtA.t[:, 0:nh * 128], rtB.t[:, 0:nh * 128], ALU.add),
             reads=rtA.T + rtB.T, writes=rtA.T)

    def halo_kv(s3, col0, vslot):
        b = ps1()
        for kc in range(8):
            S.op("pe", lambda e, kc=kc: e.matmul(pst[:, b, 0:128], wst(s3, kc, 0), xnT.t[:, kc, col0:col0 + 128],
                                                 start=(kc == 0), stop=(kc == 7)),
                 reads=[ring.T[s3], xnT.T[col0 // 128]], writes=[PB[b]], signal=(kc == 7))
        S.op("act", lambda e: e.copy(akT.t[:, 0:128], pst[:, b, 0:128]), reads=[PB[b]], writes=akT.T)
        av_tok(s3, col0, vslot)

    def av_tok(s3, col0, vslot):
        b = ps1()
        for kc in range(8):
            S.op("pe", lambda e, kc=kc: e.matmul(pst[:, b, 0:128], xnT.t[:, kc, col0:col0 + 128], wst(s3, kc, 128),
                                                 start=(kc == 0), stop=(kc == 7)),
                 reads=[ring.T[s3], xnT.T[col0 // 128]], writes=[PB[b]], signal=(kc == 7))
        for kvh in range(2):
            S.op("act", lambda e, kvh=kvh: e.copy(vbuf.t[:, vslot, kvh, kvh * 64:(kvh + 1) * 64],
                                                  pst[:, b, kvh * 64:(kvh + 1) * 64]),
                 reads=[PB[b]], writes=[vbuf.T[vslot]])

    if 0 not in phases:
        while prep_rest:
            prep_block(prep_rest.pop(0))
    if 0 in phases:
        s_bk = load_block(5)
        s_bv = load_block(6)
        s_b3 = load_block(3)
        bst = ps1()
        reserved.add(bst)
        NPRE = dbg.get('npre', NT_PRE)
        pbanks = {}

        def pre_load(t):
            S.dma(cq, xpb.t[:, t % 3], xp.ap()[t * 128:(t + 1) * 128, :], writes=[xpb.T[t % 3]])
            S.dma(cq, cs.t[:, t % 4], csp_d.ap()[:, t, :], writes=[cs.T[t % 4]])

        def pre_stage(sg, t):
            k3 = t % 4
            k2 = t % 3
            if sg == 0:
                norm_a(xpb.t[:, k2], [xpb.T[k2]], t)
            elif sg == 1:
                norm_b(0, t)
            elif sg == 2:
                bk_ = ps1()
                reserved.add(bk_)
                bv_ = ps1()
                reserved.add(bv_)
                pbanks[t] = (bk_, bv_)
                for kc in range(8):
                    S.op("pe", lambda e, kc=kc: e.matmul(pst[:, bk_, :],
                                                         xnT.t[:, kc, 0:128], wmv(s_bk, kc), start=(kc == 0), stop=(kc == 7)),
                         reads=[ring.T[s_bk]] + xnT.T, writes=[PB[bk_]], signal=(kc == 7))
                for kc in range(8):
                    S.op("pe", lambda e, kc=kc: e.matmul(pst[:, bv_, :],
                                                         xnT.t[:, kc, 0:128], wmv(s_bv, kc), start=(kc == 0), stop=(kc == 7)),
                         reads=[ring.T[s_bv]] + xnT.T, writes=[PB[bv_]], signal=(kc == 7))
                if t == NPRE - 1:
                    halo_kv(s_b3, 0, (0 - 1) % 5)
            elif sg == 3:
                bk_, bv_ = pbanks.pop(t)
                rotary(pst[:, bk_, :].rearrange("p (h d) -> p h d", h=4), 4, k3 * 192, [cs.T[k3]], [PB[bk_]])
                kdb = bass.AP(kdp.t, t * 4, [[NT_PRE * 4, 128], [1, 4], [0, 128]])
                S.op("pool", lambda e: e.tensor_tensor(kd.t[:, 0].rearrange("p (h d) -> p h d", h=4),
                                                       rtA.t[:, 0:512].rearrange("p (h d) -> p h d", h=4), kdb, ALU.mult),
                     reads=rtA.T + kdp.T, writes=[kd.T[0]])
                S.op("act", lambda e: e.copy(vtok.t[:, 0], pst[:, bv_, :]), reads=[PB[bv_]], writes=[vtok.T[0]])
                reserved.discard(bk_)
                reserved.discard(bv_)
            else:
                for hh in range(4):
                    S.op("pe", lambda e, hh=hh: e.matmul(pst[:, bst, hh * 128:(hh + 1) * 128], kd.t[:, 0, hh * 128:(hh + 1) * 128],
                                                         vtok.t[:, 0, hh * 128:(hh + 1) * 128],
                                                         start=(t == 0 and hh == 0), stop=(t == NPRE - 1)),
                         reads=[kd.T[0], vtok.T[0]], writes=[PB[bst]], signal=(hh == 3))

        pre_load(0)
        for step in range(NPRE + 4):
            if prep_rest:
                prep_block(prep_rest.pop(0))
            for sg in (4, 3, 2, 1, 0):
                t = step - sg
                if 0 <= t < NPRE:
                    pre_stage(sg, t)
            if step + 1 < NPRE:
                pre_load(step + 1)
        S.op("dve", lambda e: e.tensor_copy(stf.t[:], pst[:, bst, :]), reads=[PB[bst]], writes=stf.T)
        S.op("act", lambda e: e.copy(stb.t[:], pst[:, bst, :]), reads=[PB[bst]], writes=stb.T)
        reserved.discard(bst)
        release(s_bk, s_bv, s_b3)

    while prep_rest:
        prep_block(prep_rest.pop(0))
    for mt in range(2):
        S.dma(cq, xpb.t[:, mt], memd.ap()[mt * 128:(mt + 1) * 128, :], writes=[xpb.T[mt]])
    for mt in range(2):
        rmsnorm_T(xpb.t[:, mt], [xpb.T[mt]], mt * 128, mt)
    memT = Buf(fmY.t[:, :, 0:256], fmY.T)
    S.op("dve", lambda e: e.tensor_copy(memT.t, xnT.t[:, :, 0:256]), reads=xnT.T, writes=memT.T)
    for l in range(2):
        for j in range(2):
            s = load_block(27 + 4 * l + j)
            for fc in range(4):
                b = ps1()
                for kc in range(8):
                    S.op("pe", lambda e, kc=kc, fc=fc: e.matmul(pst[:, b, 0:256], wst(s, kc, fc * 128), memT.t[:, kc, :],
                                                                start=(kc == 0), stop=(kc == 7)),
                         reads=[ring.T[s]] + memT.T, writes=[PB[b]], signal=(kc == 7))
                S.op("act", lambda e, fc=fc: e.copy(KmT.t[:, l, 4 * j + fc, :], pst[:, b, 0:256]),
                     reads=[PB[b]], writes=[KmT.T[l]])
            release(s)
        for j in range(2):
            s = load_block(27 + 4 * l + 2 + j)
            for mc in range(2):
                b = ps1()
                for kc in range(8):
                    S.op("pe", lambda e, kc=kc, mc=mc: e.matmul(pst[:, b, :],
                                                                memT.t[:, kc, mc * 128:(mc + 1) * 128], wmv(s, kc),
                                                                start=(kc == 0), stop=(kc == 7)),
                         reads=[ring.T[s]] + memT.T, writes=[PB[b]], signal=(kc == 7))
                S.op("act", lambda e, mc=mc: e.copy(Vm.t[:, l, mc, j * 512:(j + 1) * 512], pst[:, b, :]),
                     reads=[PB[b]], writes=[Vm.T[l]])
            release(s)

    S.op("pool", lambda e: e.memset(zbuf.t[:], 0.0), writes=zbuf.T)
    out_ops = []
    nsidx = [0]

    prenormed = [False]
    early = {"a": [], "b": [], "sid": {}}

    def early_a(j, ap_, t_):
        nsidx[0] += 1
        early["sid"][j] = nsidx[0]
        norm_a(ap_, [t_], nsidx[0])
        early["a"].append(j)

    def early_b(j):
        norm_b(j * 128, early["sid"][j])
        early["b"].append(j)

    def norm_group(nt):
        if prenormed[0]:
            prenormed[0] = False
            return
        for i in range(nt):
            if i not in early["a"]:
                early_a(i, xa(i), xT(i))
            if i >= 1 and (i - 1) not in early["b"]:
                early_b(i - 1)
        if (nt - 1) not in early["b"]:
            early_b(nt - 1)
        early["a"], early["b"], early["sid"] = [], [], {}

    def make_prenorm(nt):
        pend = []

        def post(i):
            if pend:
                j, sj = pend.pop()
                norm_b(j * 128, sj)
            nsidx[0] += 1
            norm_a(xa(i), [xT(i)], nsidx[0])
            pend.append((i, nsidx[0]))
            if i == nt - 1:
                j, sj = pend.pop()
                norm_b(j * 128, sj)
                prenormed[0] = True
        return post

    def phase_l0(g, t0, nt):
        N = nt * 128
        norm_group(nt)
        aq, sag = fmA.t[:, 0:4], fmA.t[:, 4:8]
        s_aq = load_block(0)
        s3 = load_block(3)
        s_q = load_block(4)
        s_k = load_block(5)
        s_v = load_block(6)
        for fc in range(4):
            proj_fm(s_aq, fc * 128, N, lambda p, Tb, fc=fc: S.op(
                "act", lambda e: e.copy(aq[:, fc, 0:N], p), reads=Tb, writes=[fmA.T[0]]))
        release(s_aq)
        proj_fm(s3, 0, N, lambda p, Tb: S.op(
            "act", lambda e: e.copy(akT.t[:, 128:128 + N], p), reads=Tb, writes=akT.T))
        for i in range(nt):
            av_tok(s3, i * 128, (t0 + i) % 5)
        release(s3)

        def tok_proj(sl, bank, i):
            for kc in range(8):
                S.op("pe", lambda e, kc=kc: e.matmul(pst[:, bank, :],
                                                     xnT.t[:, kc, i * 128:(i + 1) * 128], wmv(sl, kc),
                                                     start=(kc == 0), stop=(kc == 7)),
                     reads=[ring.T[sl], xnT.T[i]], writes=[PB[bank]], signal=(kc == 7))

        def stageA(i):
            gt = t0 + i
            par = gt % 2
            k3 = gt % 3
            S.dma(cq, cs.t[:, k3], csm_d.ap()[:, gt, :], writes=[cs.T[k3]])
            bq = ps2()
            reserved.update((bq, bq + 1))
            tok_proj(s_q, bq, i)
            yield
            tok_proj(s_k, bq + 1, i)
            yield
            bv_ = ps1()
            tok_proj(s_v, bv_, i)
            rotary(pst[:, bq:bq + 2, :].rearrange("p a (h d) -> p (a h) d", h=4), 8, k3 * 192, [cs.T[k3]],
                   [PB[bq], PB[bq + 1]])
            reserved.difference_update((bq, bq + 1))
            S.op("act", lambda e: e.copy(qkr.t[:, par], rtA.t[:]), reads=rtA.T, writes=[qkr.T[par]])
            kdb = bass.AP(kdtab.t, 0, [[4, 128], [1, 4], [0, 128]])
            S.op("pool", lambda e: e.tensor_tensor(kd.t[:, par].rearrange("p (h d) -> p h d", h=4),
                                                   rtA.t[:, 512:1024].rearrange("p (h d) -> p h d", h=4), kdb, ALU.mult),
                 reads=rtA.T + kdtab.T, writes=[kd.T[par]])
            S.op("act", lambda e: e.copy(vtok.t[:, par], pst[:, bv_, :]), reads=[PB[bv_]], writes=[vtok.T[par]])

        def swa_scores(i, hs):
            bs = ps2()
            p0 = hs * 64
            for fc in range(4):
                for blk in range(2):
                    kcol = (i + blk) * 128
                    S.op("pe", lambda e, fc=fc, blk=blk, kcol=kcol: e.matmul(
                        pst[:, bs + fc // 2, ((fc % 2) * 2 + blk) * 128:((fc % 2) * 2 + blk + 1) * 128],
                        akT.t[p0:p0 + 64, kcol:kcol + 128], aq[p0:p0 + 64, fc, i * 128:(i + 1) * 128],
                        start=True, stop=True),
                        reads=akT.T + [fmA.T[0]], writes=[PB[bs + fc // 2]], signal=(fc % 2 == 1 and blk == 1))
            src = pst[:, bs:bs + 2, :].rearrange("p a (f b n) -> p (a f) b n", f=2, b=2)
            dst = PT.t[:, 4 * hs:4 * hs + 4]
            srcf = pst[:, bs:bs + 2, :]
            dstf = PT.t[:, 4 * hs:4 * hs + 4].rearrange("p f b n -> p (f b) n").rearrange("p (a x) n -> p a (x n)", a=2)
            if g == 0 and i == 0:
                S.op("act", lambda e: e.activation(dst[:, :, 0, :], src[:, :, 0, :], AF.Exp, bias=hbias.t[:, 0:1], scale=0.125),
                     reads=[PB[bs], PB[bs + 1]] + hbias.T, writes=[PT.T[hs]])
                S.op("act", lambda e: e.activation(dst[:, :, 1, :], src[:, :, 1, :], AF.Exp, scale=0.125),
                     reads=[PB[bs], PB[bs + 1]], writes=[PT.T[hs]])
            else:
                S.op("act", lambda e: e.activation(dstf, srcf, AF.Exp, scale=0.125),
                     reads=[PB[bs], PB[bs + 1]], writes=[PT.T[hs]])

        def swa_pv(i, vs, vsp):
            rd3 = rd.t[:].rearrange("p (f n) -> p f n", f=4)
            for hs in range(2):
                bank = ps1()
                for fc in range(4):
                    h = 4 * hs + fc
                    for qc in range(2):
                        q0 = qc * 64
                        if qc == 0:
                            segs = [(vsp, 0, 0, 128), (vs, 1, 0, 64)]
                        else:
                            segs = [(vs, 1, 0, 128), (vsp, 0, 64, 128)]
                        for si, (slot, blk, k0, k1) in enumerate(segs):
                            S.op("pe", lambda e, slot=slot, blk=blk, k0=k0, k1=k1, si=si, h=h, q0=q0, fc=fc: e.matmul(
                                pst[:, bank, fc * 128 + q0:fc * 128 + q0 + 64], vbuf.t[k0:k1, slot, hs, :],
                                PT.t[k0:k1, h, blk, q0:q0 + 64], start=(si == 0), stop=(si == 1)),
                                reads=[vbuf.T[slot], PT.T[hs]], writes=[PB[bank]],
                                signal=(si == 1 and fc == 3 and qc == 1))
                p3 = pst[:, bank, :].rearrange("p (f n) -> p f n", f=4)
                lo, hi = (0, 64) if hs == 0 else (64, 128)
                dl, dh = (64, 128) if hs == 0 else (0, 64)
                esb = bass.AP(esink.t, lo * 4, [[4, 64], [1, 4], [0, 128]])
                S.op("dve", lambda e: e.tensor_tensor(rd3[lo:hi], p3[dl:dh], esb, ALU.add),
                     reads=[PB[bank]] + esink.T, writes=[rdh[hs]])
                S.op("act", lambda e: e.activation(rd3[lo:hi], rd3[lo:hi], AF.Ln), reads=[rdh[hs]], writes=[rdh[hs]])
                S.op("act", lambda e: e.activation(rd3[lo:hi], rd3[lo:hi], AF.Exp, scale=-1.0),
                     reads=[rdh[hs]], writes=[rdh[hs]])
                S.op("dve", lambda e: e.tensor_tensor(rd3[lo:hi], p3[lo:hi], rd3[lo:hi], ALU.mult),
                     reads=[PB[bank], rdh[hs]], writes=[rdh[hs]])
                S.op("pool", lambda e: e.tensor_tensor(fmY.t[lo:hi, 0:4, i * 128:(i + 1) * 128], rd3[lo:hi],
                                                       sag[lo:hi, :, i * 128:(i + 1) * 128], ALU.mult),
                     reads=[rdh[hs], fmA.T[1]], writes=fmY.T[0:4])

        def stageB(i):
            gt = t0 + i
            par = gt % 2
            vs = gt % 5
            vsp = (gt - 1) % 5
            qk_ = qkr.t[:, par]
            kd_ = kd.t[:, par]
            vt_ = vtok.t[:, par]
            bt = ps1()
            pb = psb16(bt)
            for h8 in range(8):
                S.op("pe", lambda e, h8=h8: e.transpose(pb[:, h8 * 128:(h8 + 1) * 128], qk_[:, h8 * 128:(h8 + 1) * 128],
                                                        ident.t[:]),
                     reads=[qkr.T[par]] + ident.T, writes=[PB[bt]], signal=(h8 == 7))
            S.op("dve", lambda e: e.tensor_copy(qkT.t[:].rearrange("p h n -> p (h n)"), pb[:, :]),
                 reads=[PB[bt]], writes=qkT.T)
            S.op("dve", lambda e: e.tensor_tensor(qdT.t[:].rearrange("p h n -> p (h n)"), pb[:, 0:512], qdtab.t[:], ALU.mult),
                 reads=[PB[bt]] + qdtab.T, writes=qdT.T)
            swa_scores(i, 0)
            yield
            bsc = ps1()
            for hh in range(4):
                S.op("pe", lambda e, hh=hh: e.matmul(pst[:, bsc, hh * 128:(hh + 1) * 128], qkT.t[:, 4 + hh, :],
                                                     qkT.t[:, hh, :], start=True, stop=True),
                     reads=qkT.T, writes=[PB[bsc]], signal=(hh == 3))
            S.op("dve", lambda e: e.tensor_tensor(scT.t[:], pst[:, bsc, :], dtab.t[:], ALU.mult),
                 reads=[PB[bsc]] + dtab.T, writes=scT.T)
            bkv = ps1()
            reserved.add(bkv)
            for hh in range(4):
                S.op("pe", lambda e, hh=hh: e.matmul(pst[:, bkv, hh * 128:(hh + 1) * 128], kd_[:, hh * 128:(hh + 1) * 128],
                                                     vt_[:, hh * 128:(hh + 1) * 128], start=True, stop=True),
                     reads=[kd.T[par], vtok.T[par]], writes=[PB[bkv]], signal=(hh == 3))
            swa_scores(i, 1)
            yield
            bo = ps1()
            for hh in range(4):
                S.op("pe", lambda e, hh=hh: e.matmul(pst[:, bo, hh * 128:(hh + 1) * 128], scT.t[:, hh * 128:(hh + 1) * 128],
                                                     vt_[:, hh * 128:(hh + 1) * 128], start=True, stop=False),
                     reads=scT.T + [vtok.T[par]], writes=[PB[bo]], signal=False)
                S.op("pe", lambda e, hh=hh: e.matmul(pst[:, bo, hh * 128:(hh + 1) * 128], qdT.t[:, hh, :],
                                                     stb.t[:, hh * 128:(hh + 1) * 128], start=False, stop=True),
                     reads=qdT.T + stb.T, writes=[PB[bo]], signal=(hh == 3))
            for hh in range(4):
                S.op("dve", lambda e, hh=hh: e.scalar_tensor_tensor(
                    stf.t[:, hh * 128:(hh + 1) * 128], stf.t[:, hh * 128:(hh + 1) * 128], float(GAMMA[hh] ** 128),
                    pst[:, bkv, hh * 128:(hh + 1) * 128], ALU.mult, ALU.add),
                    reads=[PB[bkv]] + stf.T, writes=stf.T)
            reserved.discard(bkv)
            S.op("act", lambda e: e.copy(stb.t[:], stf.t[:]), reads=stf.T, writes=stb.T)
            nsidx[0] += 1
            c = (nsidx[0] % 4) * 4
            Tst = [stat.T[nsidx[0] % 4]]
            for hh in range(4):
                S.op("act", lambda e, hh=hh: e.activation(ybn.t[:, hh * 128:(hh + 1) * 128], pst[:, bo, hh * 128:(hh + 1) * 128],
                                                          AF.Square, accum_out=stat.t[:, c + hh:c + hh + 1]),
                     reads=[PB[bo]], writes=ybn.T + Tst)
            S.op("act", lambda e: e.activation(stat.t[:, c:c + 4], stat.t[:, c:c + 4], AF.Ln, bias=EPS, scale=1.0 / 128),
                 reads=Tst, writes=Tst)
            S.op("act", lambda e: e.activation(stat.t[:, c:c + 4], stat.t[:, c:c + 4], AF.Exp, scale=-0.5),
                 reads=Tst, writes=Tst)
            rb = bass.AP(stat.t, c, [[24, 128], [1, 4], [0, 128]])
            S.op("dve", lambda e: e.tensor_tensor(ybn.t[:].rearrange("p (h d) -> p h d", h=4),
                                                  pst[:, bo, :].rearrange("p (h d) -> p h d", h=4), rb, ALU.mult),
                 reads=[PB[bo]] + Tst, writes=ybn.T)
            swa_pv(i, vs, vsp)
            yield
            bt2 = ps1()
            pb2 = psb16(bt2)
            for hh in range(4):
                S.op("pe", lambda e, hh=hh: e.transpose(pb2[:, hh * 128:(hh + 1) * 128], ybn.t[:, hh * 128:(hh + 1) * 128],
                                                        ident.t[:]),
                     reads=ybn.T + ident.T, writes=[PB[bt2]], signal=(hh == 3))
            S.op("dve", lambda e: e.tensor_tensor(fmY.t[:, 4:8, i * 128:(i + 1) * 128],
                                                  pb2[:, 0:512].rearrange("p (h n) -> p h n", h=4),
                                                  sbg.t[:, :, i * 128:(i + 1) * 128], ALU.mult),
                 reads=[PB[bt2]] + sbg.T, writes=fmY.T[4:8])
            yield
            out_proj_tile(s_o0, s_o1, i, fmY)

        l0_post = make_prenorm(nt) if 1 in phases else None

        def run(gen):
            try:
                next(gen)
                return True
            except StopIteration:
                return False

        for _ in stageA(0):
            pass
        s_ag = load_block(1)
        for fc in range(4):
            proj_fm(s_ag, fc * 128, N, lambda p, Tb, fc=fc: S.op(
                "act", lambda e: e.activation(sag[:, fc, 0:N], p, AF.Silu), reads=Tb, writes=[fmA.T[1]]))
        release(s_ag)
        s_bg = load_block(2)
        for fc in range(4):
            proj_fm(s_bg, fc * 128, N, lambda p, Tb, fc=fc: S.op(
                "act", lambda e: e.activation(sbg.t[:, fc, 0:N], p, AF.Silu), reads=Tb, writes=sbg.T))
        release(s_bg)
        s_o0 = load_block(7)
        s_o1 = load_block(8)
        for i in range(nt):
            gB = stageB(i)
            gA = stageA(i + 1) if i + 1 < nt else None
            run(gB)
            if gA:
                run(gA)
            run(gB)
            if l0_post is not None and i > 0:
                l0_post(i - 1)
            run(gB)
            run(gB)
            if gA:
                run(gA)
                run(gA)
                run(gA)
            else:
                release(s_q, s_k, s_v)
            run(gB)
            run(gB)
        release(s_o0, s_o1)
        if l0_post is not None:
            l0_post(nt - 1)
        S.op("pool", lambda e: e.tensor_copy(akT.t[:, 0:128], akT.t[:, N:N + 128]), reads=akT.T, writes=akT.T)

    def phase_xattn(l, nt, post=None):
        N = nt * 128
        norm_group(nt)
        qx = fmA.t
        for j in range(2):
            s = load_block((9 if l == 0 else 23) + j)
            for fc in range(4):
                proj_fm(s, fc * 128, N, lambda p, Tb, fc=fc: S.op(
                    "act", lambda e: e.copy(qx[:, 4 * j + fc, 0:N], p), reads=Tb, writes=[fmA.T[j]]))
            release(s)
        sbanks = {}

        def scores(hx):
            bs = ps2()
            sbanks[hx] = bs
            for mc in range(2):
                for dc in range(2):
                    c = 2 * hx + dc
                    S.op("pe", lambda e, mc=mc, dc=dc, c=c: e.matmul(pst[:, bs + mc, 0:N], KmT.t[:, l, c, mc * 128:(mc + 1) * 128],
                                                                     qx[:, c, 0:N], start=(dc == 0), stop=(dc == 1)),
                         reads=[KmT.T[l], fmA.T[c // 4]], writes=[PB[bs + mc]], signal=(dc == 1))
            S.op("act", lambda e: e.activation(PT2.t[:, hx % 2, :, 0:N], pst[:, bs:bs + 2, 0:N], AF.Exp, scale=1.0 / 16),
                 reads=[PB[bs], PB[bs + 1]], writes=[PT2.T[hx % 2]])

        def pv(hx):
            bn0 = ps1()
            bn1 = ps1()
            bd = ps1()
            for mc in range(2):
                S.op("pe", lambda e, mc=mc: e.matmul(pst[:, bd, 0:N], ones.t[:, 2, :], PT2.t[:, hx % 2, mc, 0:N],
                                                     start=(mc == 0), stop=(mc == 1)),
                     reads=ones.T + [PT2.T[hx % 2]], writes=[PB[bd]], signal=(mc == 1))
            for dc, bn in enumerate((bn0, bn1)):
                for mc in range(2):
                    S.op("pe", lambda e, dc=dc, mc=mc, bn=bn: e.matmul(
                        pst[:, bn, 0:N], Vm.t[:, l, mc, (2 * hx + dc) * 128:(2 * hx + dc + 1) * 128],
                        PT2.t[:, hx % 2, mc, 0:N], start=(mc == 0), stop=(mc == 1)),
                        reads=[Vm.T[l], PT2.T[hx % 2]], writes=[PB[bn]], signal=(mc == 1))
            hp = hx % 2
            S.op("dve", lambda e: e.reciprocal(rdx[:, hp, 0:N], pst[:, bd, 0:N]), reads=[PB[bd]], writes=[rdx_T[hp]])
            if hx == 3:
                for dc, bn in enumerate((bn0, bn1)):
                    S.op("dve", lambda e, dc=dc, bn=bn: e.tensor_tensor(fmY.t[:, 2 * hx + dc, 0:N], pst[:, bn, 0:N],
                                                                        rdx[:, hp, 0:N], ALU.mult),
                         reads=[PB[bn], rdx_T[hp]], writes=[fmY.T[2 * hx + dc]])
                return
            for dc, bn in enumerate((bn0, bn1)):
                S.op("act", lambda e, dc=dc, bn=bn: e.copy(numsb[:, hp, dc, 0:N], pst[:, bn, 0:N]),
                     reads=[PB[bn]], writes=[numsb_T[hp]])
            for dc in range(2):
                S.op("pool", lambda e, dc=dc: e.tensor_tensor(fmY.t[:, 2 * hx + dc, 0:N], numsb[:, hp, dc, 0:N],
                                                               rdx[:, hp, 0:N], ALU.mult),
                     reads=[numsb_T[hp], rdx_T[hp]], writes=[fmY.T[2 * hx + dc]])

        scores(0)
        for hx in range(4):
            if hx + 1 < 4:
                scores(hx + 1)
            pv(hx)
        base = 11 if l == 0 else 25
        s0 = load_block(base)
        s1 = load_block(base + 1)
        out_proj(s0, s1, nt, fmY, list(range(nt)), post)
        release(s0, s1)

    def phase_l1(nt):
        N = nt * 128
        norm_group(nt)
        for j in range(4):
            sa = load_block(13 + 2 * j)
            sb_ = load_block(14 + 2 * j)
            for f2 in range(2):
                fc = 2 * j + f2
                k = fc % 2
                Tz = [zbuf.T[fc]]
                proj_fm(sa, f2 * 128, N, lambda p, Tb: S.op(
                    "act", lambda e: e.copy(gcs.t[:, k, 0:N], p), reads=Tb, writes=[gcs.T[k]]))
                proj_fm(sa, 256 + f2 * 128, N, lambda p, Tb: S.op(
                    "dve", lambda e: e.tensor_tensor(zbuf.t[:, fc, 2:2 + N], p, gcs.t[:, k, 0:N], ALU.mult),
                    reads=Tb + [gcs.T[k]], writes=Tz))
                S.op("pool", lambda e: e.tensor_scalar(cvt.t[:, k, 0:N], zbuf.t[:, fc, 2:2 + N], cw.t[:, 2, fc:fc + 1],
                                                       cw.t[:, 3, fc:fc + 1], ALU.mult, ALU.add),
                     reads=Tz + cw.T, writes=[cvt.T[k]])
                S.op("dve", lambda e: e.scalar_tensor_tensor(cvt.t[:, k, 0:N], zbuf.t[:, fc, 1:1 + N], cw.t[:, 1, fc:fc + 1],
                                                              cvt.t[:, k, 0:N], ALU.mult, ALU.add),
                     reads=Tz + cw.T + [cvt.T[k]], writes=[cvt.T[k]])
                S.op("dve", lambda e: e.scalar_tensor_tensor(cvt.t[:, k, 0:N], zbuf.t[:, fc, 0:N], cw.t[:, 0, fc:fc + 1],
                                                              cvt.t[:, k, 0:N], ALU.mult, ALU.add),
                     reads=Tz + cw.T + [cvt.T[k]], writes=[cvt.T[k]])
                S.op("pool", lambda e: e.tensor_copy(zbuf.t[:, fc, 0:2], zbuf.t[:, fc, N:N + 2]), reads=Tz, writes=Tz)
                proj_fm(sb_, f2 * 128, N, lambda p, Tb: S.op(
                    "dve", lambda e: e.tensor_tensor(cvt.t[:, k, 0:N], p, cvt.t[:, k, 0:N], ALU.mult),
                    reads=Tb + [cvt.T[k]], writes=[cvt.T[k]]))
                proj_fm(sb_, 256 + f2 * 128, N, lambda p, Tb: S.op(
                    "act", lambda e: e.activation(sgb.t[:, k, 0:N], p, AF.Silu), reads=Tb, writes=[sgb.T[k]]))
                S.op("pool", lambda e: e.tensor_tensor(fmY.t[:, fc, 0:N], cvt.t[:, k, 0:N], sgb.t[:, k, 0:N], ALU.mult),
                     reads=[cvt.T[k], sgb.T[k]], writes=[fmY.T[fc]])
            release(sa, sb_)
        s0 = load_block(21)
        s1 = load_block(22)
        out_proj(s0, s1, nt, fmY, list(range(nt)), make_prenorm(nt) if 3 in phases else None)
        release(s0, s1)

    maps = [{i: i for i in range(groups[0][1])}]
    for gi in range(1, len(groups)):
        prev = maps[gi - 1]
        free_now = [b for b in range(5) if b not in prev.values()]
        order = free_now + [prev[j] for j in sorted(prev)]
        maps.append({i: order[i] for i in range(groups[gi][1])})

    def x_load(gi, i):
        tt0 = groups[gi][0]
        ap_, t_ = XB[maps[gi][i]]
        S.dma(cq, ap_, xm.ap()[(tt0 + i) * 128:(tt0 + i + 1) * 128, :], writes=[t_])

    for i in range(groups[0][1]):
        x_load(0, i)
    for g, (t0, nt) in enumerate(groups):
        xmap.clear()
        xmap.update(maps[g])
        if g + 1 < len(groups):
            for i, b in maps[g + 1].items():
                if b not in maps[g].values():
                    x_load(g + 1, i)

        def finish_tile(i, g=g, t0=t0, nt=nt):
            gt = t0 + i
            if final:
                k = gt % 2
                c = 16 + 4 * k
                Tst = [stat.T[4 + k]]
                S.op("act", lambda e: e.activation(obuf.t[:, k], xa(i), AF.Square, accum_out=stat.t[:, c:c + 1]),
                     reads=[xT(i)], writes=[obuf.T[k]] + Tst)
                S.op("act", lambda e: e.activation(stat.t[:, c + 1:c + 2], stat.t[:, c:c + 1], AF.Ln, bias=EPS, scale=1.0 / D),
                     reads=Tst, writes=Tst)
                S.op("act", lambda e: e.activation(stat.t[:, c + 2:c + 3], stat.t[:, c + 1:c + 2], AF.Exp, scale=-0.5),
                     reads=Tst, writes=Tst)
                S.op("dve", lambda e: e.scalar_tensor_tensor(obuf.t[:, k], xa(i), stat.t[:, c + 2:c + 3], gfin.t[:],
                                                             ALU.mult, ALU.mult),
                     reads=[xT(i)] + Tst + gfin.T, writes=[obuf.T[k]])
                out_ops.append(S.dma(cq, out_d.ap()[gt * 128:(gt + 1) * 128, :], obuf.t[:, k], reads=[obuf.T[k]]))
            else:
                out_ops.append(S.dma(cq, out_d.ap()[gt * 128:(gt + 1) * 128, :], xa(i), reads=[xT(i)]))
            if g + 1 < len(groups):
                for j, b in maps[g + 1].items():
                    if b == maps[g][i]:
                        x_load(g + 1, j)
                if 0 in phases and nt == 4 and groups[g + 1][1] == 4:
                    if i >= 2:
                        early_b(i - 2)
                    if i >= 1:
                        ap_, t_ = XB[maps[g + 1][i - 1]]
                        early_a(i - 1, ap_, t_)

        last = max(phases)
        S.label = 'g%d_l0' % g
        if 0 in phases:
            phase_l0(g, t0, nt)
        S.label = 'g%d_xa0' % g
        if 1 in phases:
            phase_xattn(0, nt, finish_tile if last == 1 else make_prenorm(nt))
        S.label = 'g%d_l1' % g
        if 2 in phases:
            phase_l1(nt)
        S.label = 'g%d_xa1' % g
        if 3 in phases:
            phase_xattn(1, nt, finish_tile if last == 3 else None)
        if last in (0, 2):
            for i in range(nt):
                finish_tile(i)
    for d in out_ops:
        S._wait("sp", d)
        S._wait("pool", d)
    return nc, S


def _consts(half):
    start = 0 if half == 0 else (NT_PRE) * 128
    inv = 1.0 / (10000.0 ** np.linspace(0.0, 1.0, 64))

    def cstab(pos):
        ang = pos[:, None].astype(np.float64) * inv[None, :]
        c, s = np.cos(ang), np.sin(ang)
        return np.concatenate([c, s, -s], axis=1).astype(np.float32)

    csm = cstab(start + np.arange(NT_MAIN * 128)).reshape(NT_MAIN, 128, 192).transpose(1, 0, 2)
    csp = cstab(np.arange(NT_PRE * 128)).reshape(NT_PRE, 128, 192).transpose(1, 0, 2)
    sc = 128.0 ** -0.5
    i = np.arange(128)
    dt = np.zeros((128, 4, 128), np.float64)
    qd = np.zeros((128, 4, 128), np.float64)
    kdt = np.zeros((128, 4), np.float64)
    P = NT_PRE * 128
    kdp = np.zeros((128, NT_PRE, 4), np.float64)
    for h in range(4):
        g = GAMMA[h]
        ii, jj = i[None, :], i[:, None]
        same = (ii // 64) == (jj // 64)
        earlier = (jj // 64) < (ii // 64)
        m = np.where(same, g ** np.abs(ii - jj), np.where(earlier, g ** np.maximum(ii - jj, 0), 0.0))
        dt[:, h, :] = m * sc
        qd[:, h, :] = (g ** (i + 1.0))[None, :]
        kdt[:, h] = sc * g ** (127.0 - i)
        jg = (np.arange(NT_PRE)[None, :] * 128 + i[:, None]).astype(np.float64)
        kdp[:, :, h] = sc * g ** (P - 1.0 - jg)
    f = lambda a: np.ascontiguousarray(a.reshape(a.shape[0], -1) if a.ndim == 3 and a.shape[1] == 4 else a).astype(np.float32)
    return {
        "csm": np.ascontiguousarray(csm), "csp": np.ascontiguousarray(csp),
        "dtab": f(dt), "qdtab": f(qd), "kdtab": np.ascontiguousarray(kdt).astype(np.float32),
        "kdp": np.ascontiguousarray(kdp).astype(np.float32),
        "hbias": np.full((128, 1), -30000.0 if half == 0 else 0.0, np.float32),
        "ident": np.eye(128, dtype=np.float32),
    }


def make_in_maps(inp):
    f32 = lambda a: np.ascontiguousarray(np.asarray(a, dtype=np.float32))
    x = f32(inp["x"])
    mem = f32(inp["mem"])
    gl = [inp["e_norm"][0], inp["c_norm"][0], inp["o_norm"][0], inp["c_norm"][1],
          inp["c_mem_norm"][0], inp["c_mem_norm"][1]]
    gains = np.stack([f32(g).reshape(8, 128).T for g in gl], axis=1)
    gfin = np.ascontiguousarray(np.broadcast_to(f32(inp["final_norm"])[None, :], (128, D)))
    sk = f32(inp["e_sink"])[0]
    sink = np.zeros((128, 4), np.float32)
    for fc in range(4):
        sink[0:64, fc] = sk[fc]
        sink[64:128, fc] = sk[4 + fc]
    cwv = np.concatenate([f32(inp["o_conv_w"])[0], f32(inp["o_conv_b"])], axis=0)
    cw = np.ascontiguousarray(cwv.reshape(4, 8, 128).transpose(2, 0, 1))
    shared = {
        "e_w_in": f32(inp["e_w_in"][0]), "e_w_out": f32(inp["e_w_out"][0]),
        "o_w_in": f32(inp["o_w_in"][0]), "o_w_out": f32(inp["o_w_out"][0]),
        "c_wq0": f32(inp["c_wq"][0]), "c_wq1": f32(inp["c_wq"][1]),
        "c_wkv0": f32(inp["c_wkv"][0]), "c_wkv1": f32(inp["c_wkv"][1]),
        "c_wo0": f32(inp["c_wo"][0]), "c_wo1": f32(inp["c_wo"][1]),
        "gains": np.ascontiguousarray(gains), "gfin": gfin, "sink": sink, "cw": cw,
    }
    cons = [_consts(0), _consts(1)]
    maps = []
    for c in range(8):
        b, half = c // 2, c % 2
        m = dict(shared)
        m.update(cons[half])
        if half == 0:
            m["xm"] = np.ascontiguousarray(x[b, 0:NT_MAIN * 128])
            m["xp"] = np.zeros((NT_PRE * 128, D), np.float32)
        else:
            m["xm"] = np.ascontiguousarray(x[b, NT_PRE * 128:])
            m["xp"] = np.ascontiguousarray(x[b, 0:NT_PRE * 128])
        m["mem"] = np.ascontiguousarray(mem[b])
        maps.append(m)
    return maps


def assemble(results, B=4, SEQ=8192):
    out = np.empty((B, SEQ, D), np.float32)
    for c in range(8):
        b, half = c // 2, c % 2
        o = results[c]["out"]
        if half == 0:
            out[b, 0:4096] = o[0:4096]
        else:
            out[b, 4096:] = o[128:]
    return out


_CACHE = {}


def kernel(**inputs):
    if "nc" not in _CACHE:
        _CACHE["nc"] = build()[0]
    nc = _CACHE["nc"]
    maps = make_in_maps(inputs)
    res = run_bass_kernel_spmd(nc, maps, core_ids=list(range(8)))
    return assemble(res.results)
```

```python
import numpy as np
import concourse.bass as bass
import concourse.mybir as mybir
from concourse.bass_utils import run_bass_kernel_spmd

F32 = mybir.dt.float32
BF16 = mybir.dt.bfloat16
AF = mybir.ActivationFunctionType
ALU = mybir.AluOpType

NT_MAIN = 33
NT_PRE = 31
D = 1024
EPS = 1e-6
GROUPS = [(0, 1)] + [(1 + 4 * i, 4) for i in range(8)]
NSLOT = 6


class T:
    __slots__ = ("name", "writes", "reads", "sem", "cnt", "alias", "excl", "last_read")

    def __init__(self, name, excl=False):
        self.name = name
        self.excl = excl
        self.writes = []
        self.reads = []
        self.sem = None
        self.cnt = 0
        self.alias = []
        self.last_read = 0


class Op:
    __slots__ = ("eng", "sem", "val")

    def __init__(self, eng, sem, val):
        self.eng = eng
        self.sem = sem
        self.val = val


def _compress(lst):
    last = {}
    for o in lst:
        k = id(o.sem)
        if k not in last or last[k].val < o.val:
            last[k] = o
    return list(last.values())


class Sched:
    COMPUTE = ("pe", "act", "dve", "pool")

    def __init__(self, nc):
        self.nc = nc
        self.E = {"pe": nc.tensor, "act": nc.scalar, "dve": nc.vector,
                  "pool": nc.gpsimd, "sp": nc.sync}
        self.sem = {e: nc.alloc_semaphore("s_" + e) for e in self.COMPUTE}
        self.cnt = {e: 0 for e in self.COMPUTE}
        self.waited = {e: {} for e in self.E}
        self.nops = {e: 0 for e in self.E}
        self.nwaits = 0
        self.opidx = 0
        self.label = 'init'
        self.pe_labels = []

    def _wait(self, eng, d):
        w = self.waited[eng]
        key = id(d.sem)
        if w.get(key, 0) >= d.val:
            return
        w[key] = d.val
        self.E[eng].wait_ge(d.sem, d.val)
        self.nwaits += 1

    def _deps(self, eng, reads, writes, is_dma):
        deps = []
        for t in reads:
            for d in t.writes:
                if (not is_dma) and d.eng == eng and eng == "pe":
                    continue
                deps.append(d)
        for t in writes:
            for tt in [t] + t.alias:
                for d in tt.reads:
                    if (not is_dma) and d.eng == eng:
                        continue
                    deps.append(d)
                for d in tt.writes:
                    if (not is_dma) and d.eng == eng:
                        continue
                    deps.append(d)
        return deps

    def _record(self, op, reads, writes):
        self.opidx += 1
        for t in reads:
            t.last_read = self.opidx
            t.reads.append(op)
            if len(t.reads) > 48:
                t.reads = _compress(t.reads)
        for t in writes:
            for a in t.alias:
                a.reads = []
                a.writes = []
            if t.reads:
                t.writes = [op]
                t.reads = []
            else:
                t.writes.append(op)
                if len(t.writes) > 48:
                    t.writes = _compress(t.writes)

    def op(self, eng, fn, reads=(), writes=(), signal=True):
        ex = [t for t in reads if t.excl]
        if ex:
            reads = [t for t in reads if not t.excl]
            writes = list(writes) + ex
        for d in self._deps(eng, reads, writes, False):
            self._wait(eng, d)
        ins = fn(self.E[eng])
        self.nops[eng] += 1
        if eng == 'pe':
            self.pe_labels.append(self.label)
        if signal:
            self.cnt[eng] += 1
            ins.then_inc(self.sem[eng], 1)
            op = Op(eng, self.sem[eng], self.cnt[eng])
        else:
            op = Op(eng, self.sem[eng], self.cnt[eng] + 1)
        self._record(op, reads, writes)
        return op

    def dma(self, q, out_ap, in_ap, reads=(), writes=(), semt=None, **kw):
        for d in self._deps(q, reads, writes, True):
            self._wait(q, d)
        if semt is None:
            semt = writes[0] if writes else reads[0]
        if semt.sem is None:
            semt.sem = self.nc.alloc_semaphore("d_" + semt.name)
        ins = self.E[q].dma_start(out=out_ap, in_=in_ap, **kw)
        semt.cnt += 16
        ins.then_inc(semt.sem, 16)
        op = Op("dma", semt.sem, semt.cnt)
        self.nops[q] += 1
        self._record(op, reads, writes)
        return op


class Buf:
    def __init__(self, t, tiles):
        self.t = t
        self.T = tiles


def _blocks():
    B = []
    perm = []
    for fc in range(4):
        perm += [(fc * 64, 64), ((4 + fc) * 64, 64)]
    B.append(("e_w_in", 0, "e_norm", [(c0, n) for (c0, n) in perm]))
    B.append(("e_w_in", 0, "e_norm", [(768 + c0, n) for (c0, n) in perm]))
    B.append(("e_w_in", 0, "e_norm", [(2816, 512)]))
    B.append(("e_w_in", 0, "e_norm", [(512, 256), (512, 256)]))
    B.append(("e_w_in", 0, "e_norm", [(1280, 512)]))
    B.append(("e_w_in", 0, "e_norm", [(1792, 512)]))
    B.append(("e_w_in", 0, "e_norm", [(2304, 512)]))
    B.append(("e_w_out", 0, "PERM", [(0, 512)]))
    B.append(("e_w_out", 0, "PERM", [(512, 512)]))
    B.append(("c_wq", 0, "c_norm0", [(0, 512)]))
    B.append(("c_wq", 0, "c_norm0", [(512, 512)]))
    B.append(("c_wo", 0, None, [(0, 512)]))
    B.append(("c_wo", 0, None, [(512, 512)]))
    for j in range(4):
        B.append(("o_w_in", 0, "o_norm", [(1024 + 256 * j, 256), (2048 + 256 * j, 256)]))
        B.append(("o_w_in", 0, "o_norm", [(256 * j, 256), (3072 + 256 * j, 256)]))
    B.append(("o_w_out", 0, None, [(0, 512)]))
    B.append(("o_w_out", 0, None, [(512, 512)]))
    B.append(("c_wq", 1, "c_norm1", [(0, 512)]))
    B.append(("c_wq", 1, "c_norm1", [(512, 512)]))
    B.append(("c_wo", 1, None, [(0, 512)]))
    B.append(("c_wo", 1, None, [(512, 512)]))
    for l in range(2):
        for j in range(4):
            B.append(("c_wkv", l, "c_mem_norm%d" % l, [(512 * j, 512)]))
    return B


BLOCKS = _blocks()
NBLK_STREAM = 27
GAMMA = [1.0 - 2.0 ** (-5.0 - h) for h in range(4)]


def build(phases=(0, 1, 2, 3), final=True, groups=None, stage=9, nprep=None, dbg=None):
    dbg = dbg or {}
    nc = bass.Bass("TRN2", target_bir_lowering=False)
    S = Sched(nc)
    groups = GROUPS if groups is None else groups

    def din(name, shape, dt=F32):
        return nc.dram_tensor(name, list(shape), dt, kind="ExternalInput")

    xm = din("xm", [NT_MAIN * 128, D])
    xp = din("xp", [NT_PRE * 128, D])
    memd = din("mem", [256, D])
    W = {
        "e_w_in": [din("e_w_in", [D, 3328])],
        "e_w_out": [din("e_w_out", [D, D])],
        "o_w_in": [din("o_w_in", [D, 4096])],
        "o_w_out": [din("o_w_out", [D, D])],
        "c_wq": [din("c_wq0", [D, D]), din("c_wq1", [D, D])],
        "c_wkv": [din("c_wkv0", [D, 2048]), din("c_wkv1", [D, 2048])],
        "c_wo": [din("c_wo0", [D, D]), din("c_wo1", [D, D])],
    }
    gains_d = din("gains", [128, 6, 8])
    gfin_d = din("gfin", [128, D])
    sink_d = din("sink", [128, 4])
    cw_d = din("cw", [128, 4, 8])
    csm_d = din("csm", [128, NT_MAIN, 192])
    csp_d = din("csp", [128, NT_PRE, 192])
    dtab_d = din("dtab", [128, 512])
    qdtab_d = din("qdtab", [128, 512])
    kdtab_d = din("kdtab", [128, 4])
    kdp_d = din("kdp", [128, NT_PRE, 4])
    hbias_d = din("hbias", [128, 1])
    ident_d = din("ident", [128, 128])
    out_d = nc.dram_tensor("out", [NT_MAIN * 128, D], F32, kind="ExternalOutput")
    wsc = nc.dram_tensor("wsc", [len(BLOCKS), 128, 4096], BF16)
    GIDX = {"e_norm": 0, "c_norm0": 1, "o_norm": 2, "c_norm1": 3, "c_mem_norm0": 4, "c_mem_norm1": 5}

    def sb(name, shape, dt, ntiles=1):
        t = nc.alloc_sbuf_tensor("sb_" + name, list(shape), dt)
        return Buf(t, [T(name + str(i)) for i in range(ntiles)])

    xbuf = sb("xbuf", [128, 4, D], F32, 4)
    obuf = sb("obuf", [128, 2, D], F32, 2)
    ring = sb("ring", [128, NSLOT, 8, 512], BF16, NSLOT)
    xnT = sb("xnT", [128, 8, 512], BF16, 4)
    xs = sb("xs", [128, 2, D], BF16, 2)
    stat = sb("stat", [128, 24], F32, 6)
    fmA = sb("fmA", [128, 8, 512], BF16, 2)
    fmY = sb("fmY", [128, 8, 512], BF16, 8)
    sbg = sb("sbg", [128, 4, 512], BF16)
    akT = sb("akT", [128, 640], BF16)
    vbuf = sb("vbuf", [128, 5, 2, 128], BF16, 5)
    rtA = sb("rtA", [128, D], F32)
    rtB = sb("rtB", [128, D], F32)
    qkr = sb("qkr", [128, 2, D], BF16, 2)
    kd = sb("kd", [128, 2, 512], BF16, 2)
    vtok = sb("vtok", [128, 2, 512], BF16, 2)
    qkT = sb("qkT", [128, 8, 128], BF16)
    qdT = sb("qdT", [128, 4, 128], BF16)
    scT = sb("scT", [128, 512], BF16)
    stf = sb("stf", [128, 512], F32)
    stb = sb("stb", [128, 512], BF16)
    ybn = sb("ybn", [128, 512], BF16)
    PT = sb("PT", [128, 8, 2, 128], BF16, 2)
    rd = sb("rd", [128, 512], F32)
    PT2 = Buf(PT.t[:].rearrange("p h b n -> p (h b n)").rearrange("p (a m n) -> p a m n", a=2, m=2), PT.T)
    KmT = sb("KmT", [128, 2, 8, 256], BF16, 2)
    Vm = sb("Vm", [128, 2, 2, D], BF16, 2)
    zbuf = sb("zbuf", [128, 8, 516], F32, 8)
    gcs = sb("gcs", [128, 2, 512], F32, 2)
    cvt = sb("cvt", [128, 2, 512], F32, 2)
    sgb = sb("sgb", [128, 2, 512], F32, 2)
    gains = sb("gains", [128, 6, 8], F32)
    gfin = sb("gfin", [128, D], F32)
    esink = sb("esink", [128, 4], F32)
    cw = sb("cw", [128, 4, 8], F32)
    cs = sb("cs", [128, 4, 192], F32, 4)
    dtab = sb("dtab", [128, 512], F32)
    qdtab = sb("qdtab", [128, 512], F32)
    kdtab = sb("kdtab", [128, 4], F32)
    kdp = sb("kdp", [128, NT_PRE, 4], F32, 1)
    hbias = sb("hbias", [128, 1], F32)
    identf = Buf(rd.t[:, 0:128], rd.T)
    ident = sb("ident", [128, 128], BF16)
    ones = sb("ones", [128, 3, 128], BF16)
    xpb = sb("xpb", [128, 3, D], F32, 3)

    numsb = xpb.t[:, 0].bitcast(BF16).rearrange("p (a d n) -> p a d n", a=2, d=2)
    rdx = xpb.t[:, 1].rearrange("p (a n) -> p a n", a=2)
    numsb_T = [T("numsb0"), T("numsb1")]
    rdx_T = [T("rdx0"), T("rdx1")]
    for tt in numsb_T:
        tt.alias = [xpb.T[0]]
    for tt in rdx_T:
        tt.alias = [xpb.T[1]]
    xpb.T[0].alias = list(numsb_T)
    xpb.T[1].alias = list(rdx_T)

    rdh = [T("rdh0"), T("rdh1")]
    for tt in rdh:
        tt.alias = [rd.T[0]]
    rd.T[0].alias = list(rdh)
    x5T = T("x5")
    x5T.alias = [xpb.T[2]]
    xpb.T[2].alias = [x5T]
    XB = [(xbuf.t[:, j], xbuf.T[j]) for j in range(4)] + [(xpb.t[:, 2], x5T)]
    xmap = {}

    def xa(i):
        return XB[xmap[i]][0]

    def xT(i):
        return XB[xmap[i]][1]

    pst = nc.alloc_psum_tensor("ps", [128, 8, 512], F32)
    PB = [T("psb%d" % i, excl=True) for i in range(8)]
    prr = [0]
    reserved = set()

    def ps1():
        while True:
            i = prr[0] % 8
            prr[0] += 1
            if i not in reserved:
                return i

    def ps2():
        while True:
            if prr[0] % 2:
                prr[0] += 1
            i = prr[0] % 8
            prr[0] += 2
            if i not in reserved and (i + 1) not in reserved:
                return i

    def psb16(i):
        return pst[:, i, :].bitcast(BF16)

    def bcast(buf, off, dims):
        t = buf.t
        prow = int(np.prod(t.shape[1:]))
        return bass.AP(t, off, [[prow, 128]] + [[s, c] for (s, c) in dims])

    cq = "pool"
    S.dma(cq, gains.t[:], gains_d.ap(), writes=gains.T)
    S.dma(cq, gfin.t[:], gfin_d.ap(), writes=gfin.T)
    S.dma(cq, esink.t[:], sink_d.ap(), writes=esink.T)
    S.dma(cq, cw.t[:], cw_d.ap(), writes=cw.T)
    S.dma(cq, dtab.t[:], dtab_d.ap(), writes=dtab.T)
    S.dma(cq, qdtab.t[:], qdtab_d.ap(), writes=qdtab.T)
    S.dma(cq, kdtab.t[:], kdtab_d.ap(), writes=kdtab.T)
    S.dma(cq, hbias.t[:], hbias_d.ap(), writes=hbias.T)
    S.dma(cq, kdp.t[:], kdp_d.ap(), writes=kdp.T)
    S.dma(cq, identf.t, ident_d.ap(), writes=identf.T)
    S.op("dve", lambda e: e.tensor_copy(ident.t[:], identf.t), reads=identf.T, writes=ident.T)
    S.op("act", lambda e: e.activation(esink.t[:], esink.t[:], AF.Exp), reads=esink.T, writes=esink.T)
    S.op("pool", lambda e: e.memset(ones.t[:], 0.0), writes=ones.T)
    S.op("pool", lambda e: e.memset(ones.t[:, 0, 0:64], 1.0), writes=ones.T)
    S.op("pool", lambda e: e.memset(ones.t[:, 1, 64:128], 1.0), writes=ones.T)
    S.op("pool", lambda e: e.memset(ones.t[:, 2, :], 1.0), writes=ones.T)
    S.op("pool", lambda e: e.memset(vbuf.t[:], 1.0), writes=vbuf.T)
    S.op("dve", lambda e: e.memset(akT.t[:], 0.0), writes=akT.T)

    wsc_T = [T("wsc%d" % b) for b in range(len(BLOCKS))]
    STG = [(xbuf.t[:, 0:4].rearrange("p a (kc n) -> p (a kc) n", kc=2), xbuf.T),
           (zbuf.t[:, :, 0:512], zbuf.T)]
    OST = [(obuf.t[:].rearrange("p a n -> p (a n)").bitcast(BF16), obuf.T),
           (fmA.t[:].rearrange("p c n -> p (c n)"), fmA.T)]
    prep_cnt = [0]

    def prep_block(b):
        k = prep_cnt[0]
        prep_cnt[0] += 1
        wname, l, gname, segs = BLOCKS[b]
        wd = W[wname][l]
        stt, Ts = STG[k % 2]
        ob, To = OST[k % 2]
        Ts = list(Ts)
        To = list(To)
        c = 0
        for (c0, n) in segs:
            if gname == "PERM":
                for kc in range(4):
                    for hf in range(2):
                        r0 = (kc + 4 * hf) * 64
                        S.dma("sp", stt[hf * 64:(hf + 1) * 64, kc, c:c + n],
                              wd.ap()[r0:r0 + 64, c0:c0 + n], writes=Ts, semt=Ts[0])
                src = wd.ap()[512:1024, c0:c0 + n].rearrange("(kc p) n -> p kc n", p=128)
                S.dma("sp", stt[:, 4:8, c:c + n], src, writes=Ts, semt=Ts[0])
            else:
                src = wd.ap()[:, c0:c0 + n].rearrange("(kc p) n -> p kc n", p=128)
                S.dma("sp", stt[:, :, c:c + n], src, writes=Ts, semt=Ts[0])
            c += n
        ob4 = ob.rearrange("p (k n) -> p k n", k=8)
        for kc in range(8):
            src = stt[:, kc, :]
            if gname in GIDX:
                gi = GIDX[gname]
                if kc % 2 == 0:
                    S.op("dve", lambda e, kc=kc, src=src: e.tensor_scalar_mul(ob4[:, kc, :], src, gains.t[:, gi, kc:kc + 1]),
                         reads=Ts + gains.T, writes=To)
                else:
                    S.op("act", lambda e, kc=kc, src=src: e.mul(ob4[:, kc, :], src, gains.t[:, gi, kc:kc + 1]),
                         reads=Ts + gains.T, writes=To)
            else:
                if kc % 2 == 0:
                    S.op("dve", lambda e, kc=kc, src=src: e.tensor_copy(ob4[:, kc, :], src), reads=Ts, writes=To)
                else:
                    S.op("act", lambda e, kc=kc, src=src: e.copy(ob4[:, kc, :], src), reads=Ts, writes=To)
        S.dma("act", wsc.ap()[b], ob, reads=To, writes=[wsc_T[b]], semt=To[0])

    prep_rest = list(range(27, 35)) + [b for b in range(27) if b not in (5, 6, 3)]
    for b in (5, 6, 3):
        prep_block(b)

    ring_pos = [0]

    pinned = set()

    def load_block(b):
        cands = [q for q in range(NSLOT) if q not in pinned]
        assert cands, "weight ring: all slots pinned"
        s = min(cands, key=lambda q: ring.T[q].last_read)
        pinned.add(s)
        S.dma("sp", ring.t[:, s].rearrange("p k n -> p (k n)"), wsc.ap()[b], reads=[wsc_T[b]],
              writes=[ring.T[s]])
        ring.T[s].last_read = S.opidx
        return s

    def release(*slots):
        for q in slots:
            pinned.discard(q)

    def wst(s, kc, c0, n=128):
        return ring.t[:, s, kc, c0:c0 + n]

    def wmv(s, kc):
        return ring.t[:, s, kc, :]

    def norm_a(src_ap, srcT, sidx):
        st = stat.t
        Tst = [stat.T[sidx % 4]]
        c = (sidx % 4) * 4
        k = sidx % 2
        S.op("act", lambda e: e.activation(xs.t[:, k], src_ap, AF.Square, accum_out=st[:, c:c + 1]),
             reads=srcT, writes=[xs.T[k]] + Tst)
        S.op("act", lambda e: e.activation(st[:, c + 1:c + 2], st[:, c:c + 1], AF.Ln, bias=EPS, scale=1.0 / D),
             reads=Tst, writes=Tst)
        S.op("act", lambda e: e.activation(st[:, c + 2:c + 3], st[:, c + 1:c + 2], AF.Exp, scale=-0.5),
             reads=Tst, writes=Tst)
        S.op("act", lambda e: e.mul(xs.t[:, k], src_ap, st[:, c + 2:c + 3]), reads=srcT + Tst, writes=[xs.T[k]])

    def norm_b(col0, sidx):
        k = sidx % 2
        b = ps1()
        pb = psb16(b)
        for kc in range(8):
            S.op("pe", lambda e, kc=kc: e.transpose(pb[:, kc * 128:(kc + 1) * 128], xs.t[:, k, kc * 128:(kc + 1) * 128],
                                                    ident.t[:]),
                 reads=[xs.T[k]] + ident.T, writes=[PB[b]], signal=(kc == 7))
        S.op("dve", lambda e: e.tensor_copy(xnT.t[:, :, col0:col0 + 128],
                                            pb[:, :].rearrange("p (kc n) -> p kc n", kc=8)),
             reads=[PB[b]], writes=[xnT.T[col0 // 128]])

    def rmsnorm_T(src_ap, srcT, col0, sidx):
        norm_a(src_ap, srcT, sidx)
        norm_b(col0, sidx)

    def proj_fm(s, c0, N, evac):
        b = ps1()
        for kc in range(8):
            S.op("pe", lambda e, kc=kc: e.matmul(pst[:, b, 0:N], wst(s, kc, c0), xnT.t[:, kc, 0:N],
                                                 start=(kc == 0), stop=(kc == 7)),
                 reads=[ring.T[s]] + xnT.T, writes=[PB[b]], signal=(kc == 7))
        evac(pst[:, b, 0:N], [PB[b]])

    def out_proj_tile(s0, s1, i, lhs_buf, post=None):
        b = ps2()
        for cb, s in enumerate((s0, s1)):
            for kc in range(8):
                S.op("pe", lambda e, kc=kc, cb=cb, s=s: e.matmul(
                    pst[:, b + cb, :],
                    lhs_buf.t[:, kc, i * 128:(i + 1) * 128], wmv(s, kc),
                    start=(kc == 0), stop=(kc == 7)),
                    reads=[ring.T[s], lhs_buf.T[kc]], writes=[PB[b + cb]], signal=(kc == 7))
        xi, xti = xa(i), xT(i)
        S.op("dve", lambda e: e.tensor_tensor(xi.rearrange("p (a n) -> p a n", a=2),
                                              pst[:, b:b + 2, :],
                                              xi.rearrange("p (a n) -> p a n", a=2), ALU.add),
             reads=[PB[b], PB[b + 1], xti], writes=[xti])
        if post is not None:
            post(i)

    def out_proj(s0, s1, nt, lhs_buf, xtiles, post=None):
        for i in range(nt):
            out_proj_tile(s0, s1, i, lhs_buf, post)

    def rotary(src3, nh, cst, cT, srcT):
        cs_off = cst
        cosb = bass.AP(cs.t, cs_off, [[4 * 192, 128], [0, nh * 2], [1, 64]])
        sinb = bass.AP(cs.t, cs_off + 64, [[4 * 192, 128], [0, nh], [1, 64]])
        nsinb = bass.AP(cs.t, cs_off + 128, [[4 * 192, 128], [0, nh], [1, 64]])
        A3 = rtA.t[:, 0:nh * 128].rearrange("p (h d) -> p h d", h=nh)
        B3 = rtB.t[:, 0:nh * 128].rearrange("p (h d) -> p h d", h=nh)
        src2 = src3.rearrange("p h (t d) -> p (h t) d", t=2)
        A2 = rtA.t[:, 0:nh * 128].rearrange("p (h d) -> p h d", h=nh * 2)
        S.op("dve", lambda e: e.tensor_tensor(A2, src2, cosb, ALU.mult), reads=srcT + cT, writes=rtA.T)
        S.op("dve", lambda e: e.tensor_tensor(B3[:, :, 0:64], src3[:, :, 64:128], nsinb, ALU.mult),
             reads=srcT + cT, writes=rtB.T)
        S.op("dve", lambda e: e.tensor_tensor(B3[:, :, 64:128], src3[:, :, 0:64], sinb, ALU.mult),
             reads=srcT + cT, writes=rtB.T)
        S.op("pool", lambda e: e.tensor_tensor(rtA.t[:, 0:nh * 128], rtA.t[:, 0:nh * 128], rtB.t[:, 0:nh * 128], ALU.add),
             reads=rtA.T + rtB.T, writes=rtA.T)

    def halo_kv(s3, col0, vslot):
        b = ps1()
        for kc in range(8):
            S.op("pe", lambda e, kc=kc: e.matmul(pst[:, b, 0:128], wst(s3, kc, 0), xnT.t[:, kc, col0:col0 + 128],
                                                 start=(kc == 0), stop=(kc == 7)),
                 reads=[ring.T[s3], xnT.T[col0 // 128]], writes=[PB[b]], signal=(kc == 7))
        S.op("act", lambda e: e.copy(akT.t[:, 0:128], pst[:, b, 0:128]), reads=[PB[b]], writes=akT.T)
        av_tok(s3, col0, vslot)

    def av_tok(s3, col0, vslot):
        b = ps1()
        for kc in range(8):
            S.op("pe", lambda e, kc=kc: e.matmul(pst[:, b, 0:128], xnT.t[:, kc, col0:col0 + 128], wst(s3, kc, 128),
                                                 start=(kc == 0), stop=(kc == 7)),
                 reads=[ring.T[s3], xnT.T[col0 // 128]], writes=[PB[b]], signal=(kc == 7))
        for kvh in range(2):
            S.op("act", lambda e, kvh=kvh: e.copy(vbuf.t[:, vslot, kvh, kvh * 64:(kvh + 1) * 64],
                                                  pst[:, b, kvh * 64:(kvh + 1) * 64]),
                 reads=[PB[b]], writes=[vbuf.T[vslot]])

    if 0 not in phases:
        while prep_rest:
            prep_block(prep_rest.pop(0))
    if 0 in phases:
        s_bk = load_block(5)
        s_bv = load_block(6)
        s_b3 = load_block(3)
        bst = ps1()
        reserved.add(bst)
        NPRE = dbg.get('npre', NT_PRE)
        pbanks = {}

        def pre_load(t):
            S.dma(cq, xpb.t[:, t % 3], xp.ap()[t * 128:(t + 1) * 128, :], writes=[xpb.T[t % 3]])
            S.dma(cq, cs.t[:, t % 4], csp_d.ap()[:, t, :], writes=[cs.T[t % 4]])

        def pre_stage(sg, t):
            k3 = t % 4
            k2 = t % 3
            if sg == 0:
                norm_a(xpb.t[:, k2], [xpb.T[k2]], t)
            elif sg == 1:
                norm_b(0, t)
            elif sg == 2:
                bk_ = ps1()
                reserved.add(bk_)
                bv_ = ps1()
                reserved.add(bv_)
                pbanks[t] = (bk_, bv_)
                for kc in range(8):
                    S.op("pe", lambda e, kc=kc: e.matmul(pst[:, bk_, :],
                                                         xnT.t[:, kc, 0:128], wmv(s_bk, kc), start=(kc == 0), stop=(kc == 7)),
                         reads=[ring.T[s_bk]] + xnT.T, writes=[PB[bk_]], signal=(kc == 7))
                for kc in range(8):
                    S.op("pe", lambda e, kc=kc: e.matmul(pst[:, bv_, :],
                                                         xnT.t[:, kc, 0:128], wmv(s_bv, kc), start=(kc == 0), stop=(kc == 7)),
                         reads=[ring.T[s_bv]] + xnT.T, writes=[PB[bv_]], signal=(kc == 7))
                if t == NPRE - 1:
                    halo_kv(s_b3, 0, (0 - 1) % 5)
            elif sg == 3:
                bk_, bv_ = pbanks.pop(t)
                rotary(pst[:, bk_, :].rearrange("p (h d) -> p h d", h=4), 4, k3 * 192, [cs.T[k3]], [PB[bk_]])
                kdb = bass.AP(kdp.t, t * 4, [[NT_PRE * 4, 128], [1, 4], [0, 128]])
                S.op("pool", lambda e: e.tensor_tensor(kd.t[:, 0].rearrange("p (h d) -> p h d", h=4),
                                                       rtA.t[:, 0:512].rearrange("p (h d) -> p h d", h=4), kdb, ALU.mult),
                     reads=rtA.T + kdp.T, writes=[kd.T[0]])
                S.op("act", lambda e: e.copy(vtok.t[:, 0], pst[:, bv_, :]), reads=[PB[bv_]], writes=[vtok.T[0]])
                reserved.discard(bk_)
                reserved.discard(bv_)
            else:
                for hh in range(4):
                    S.op("pe", lambda e, hh=hh: e.matmul(pst[:, bst, hh * 128:(hh + 1) * 128], kd.t[:, 0, hh * 128:(hh + 1) * 128],
                                                         vtok.t[:, 0, hh * 128:(hh + 1) * 128],
                                                         start=(t == 0 and hh == 0), stop=(t == NPRE - 1)),
                         reads=[kd.T[0], vtok.T[0]], writes=[PB[bst]], signal=(hh == 3))

        pre_load(0)
        for step in range(NPRE + 4):
            for sg in (4, 3, 2, 1, 0):
                t = step - sg
                if 0 <= t < NPRE:
                    pre_stage(sg, t)
            if step + 1 < NPRE:
                pre_load(step + 1)
            if prep_rest:
                prep_block(prep_rest.pop(0))
        S.op("dve", lambda e: e.tensor_copy(stf.t[:], pst[:, bst, :]), reads=[PB[bst]], writes=stf.T)
        S.op("act", lambda e: e.copy(stb.t[:], pst[:, bst, :]), reads=[PB[bst]], writes=stb.T)
        reserved.discard(bst)
        release(s_bk, s_bv, s_b3)

    while prep_rest:
        prep_block(prep_rest.pop(0))
    for mt in range(2):
        S.dma(cq, xpb.t[:, mt], memd.ap()[mt * 128:(mt + 1) * 128, :], writes=[xpb.T[mt]])
    for mt in range(2):
        rmsnorm_T(xpb.t[:, mt], [xpb.T[mt]], mt * 128, mt)
    memT = Buf(fmY.t[:, :, 0:256], fmY.T)
    S.op("dve", lambda e: e.tensor_copy(memT.t, xnT.t[:, :, 0:256]), reads=xnT.T, writes=memT.T)
    for l in range(2):
        for j in range(2):
            s = load_block(27 + 4 * l + j)
            for fc in range(4):
                b = ps1()
                for kc in range(8):
                    S.op("pe", lambda e, kc=kc, fc=fc: e.matmul(pst[:, b, 0:256], wst(s, kc, fc * 128), memT.t[:, kc, :],
                                                                start=(kc == 0), stop=(kc == 7)),
                         reads=[ring.T[s]] + memT.T, writes=[PB[b]], signal=(kc == 7))
                S.op("act", lambda e, fc=fc: e.copy(KmT.t[:, l, 4 * j + fc, :], pst[:, b, 0:256]),
                     reads=[PB[b]], writes=[KmT.T[l]])
            release(s)
        for j in range(2):
            s = load_block(27 + 4 * l + 2 + j)
            for mc in range(2):
                b = ps1()
                for kc in range(8):
                    S.op("pe", lambda e, kc=kc, mc=mc: e.matmul(pst[:, b, :],
                                                                memT.t[:, kc, mc * 128:(mc + 1) * 128], wmv(s, kc),
                                                                start=(kc == 0), stop=(kc == 7)),
                         reads=[ring.T[s]] + memT.T, writes=[PB[b]], signal=(kc == 7))
                S.op("act", lambda e, mc=mc: e.copy(Vm.t[:, l, mc, j * 512:(j + 1) * 512], pst[:, b, :]),
                     reads=[PB[b]], writes=[Vm.T[l]])
            release(s)

    S.op("pool", lambda e: e.memset(zbuf.t[:], 0.0), writes=zbuf.T)
    out_ops = []
    nsidx = [0]

    prenormed = [False]
    early = {"a": [], "b": [], "sid": {}}

    def early_a(j, ap_, t_):
        nsidx[0] += 1
        early["sid"][j] = nsidx[0]
        norm_a(ap_, [t_], nsidx[0])
        early["a"].append(j)

    def early_b(j):
        norm_b(j * 128, early["sid"][j])
        early["b"].append(j)

    def norm_group(nt):
        if prenormed[0]:
            prenormed[0] = False
            return
        for i in range(nt):
            if i not in early["a"]:
                early_a(i, xa(i), xT(i))
            if i >= 1 and (i - 1) not in early["b"]:
                early_b(i - 1)
        if (nt - 1) not in early["b"]:
            early_b(nt - 1)
        early["a"], early["b"], early["sid"] = [], [], {}

    def make_prenorm(nt):
        pend = []

        def post(i):
            if pend:
                j, sj = pend.pop()
                norm_b(j * 128, sj)
            nsidx[0] += 1
            norm_a(xa(i), [xT(i)], nsidx[0])
            pend.append((i, nsidx[0]))
            if i == nt - 1:
                j, sj = pend.pop()
                norm_b(j * 128, sj)
                prenormed[0] = True
        return post

    def phase_l0(g, t0, nt):
        N = nt * 128
        norm_group(nt)
        aq, sag = fmA.t[:, 0:4], fmA.t[:, 4:8]
        s_aq = load_block(0)
        s3 = load_block(3)
        s_q = load_block(4)
        s_k = load_block(5)
        s_v = load_block(6)
        for fc in range(4):
            proj_fm(s_aq, fc * 128, N, lambda p, Tb, fc=fc: S.op(
                "act", lambda e: e.copy(aq[:, fc, 0:N], p), reads=Tb, writes=[fmA.T[0]]))
        release(s_aq)
        proj_fm(s3, 0, N, lambda p, Tb: S.op(
            "act", lambda e: e.copy(akT.t[:, 128:128 + N], p), reads=Tb, writes=akT.T))
        for i in range(nt):
            av_tok(s3, i * 128, (t0 + i) % 5)
        release(s3)

        def tok_proj(sl, bank, i):
            for kc in range(8):
                S.op("pe", lambda e, kc=kc: e.matmul(pst[:, bank, :],
                                                     xnT.t[:, kc, i * 128:(i + 1) * 128], wmv(sl, kc),
                                                     start=(kc == 0), stop=(kc == 7)),
                     reads=[ring.T[sl], xnT.T[i]], writes=[PB[bank]], signal=(kc == 7))

        def stageA(i):
            gt = t0 + i
            par = gt % 2
            k3 = gt % 3
            S.dma(cq, cs.t[:, k3], csm_d.ap()[:, gt, :], writes=[cs.T[k3]])
            bq = ps2()
            reserved.update((bq, bq + 1))
            tok_proj(s_q, bq, i)
            yield
            tok_proj(s_k, bq + 1, i)
            yield
            bv_ = ps1()
            tok_proj(s_v, bv_, i)
            rotary(pst[:, bq:bq + 2, :].rearrange("p a (h d) -> p (a h) d", h=4), 8, k3 * 192, [cs.T[k3]],
                   [PB[bq], PB[bq + 1]])
            reserved.difference_update((bq, bq + 1))
            S.op("act", lambda e: e.copy(qkr.t[:, par], rtA.t[:]), reads=rtA.T, writes=[qkr.T[par]])
            kdb = bass.AP(kdtab.t, 0, [[4, 128], [1, 4], [0, 128]])
            S.op("pool", lambda e: e.tensor_tensor(kd.t[:, par].rearrange("p (h d) -> p h d", h=4),
                                                   rtA.t[:, 512:1024].rearrange("p (h d) -> p h d", h=4), kdb, ALU.mult),
                 reads=rtA.T + kdtab.T, writes=[kd.T[par]])
            S.op("act", lambda e: e.copy(vtok.t[:, par], pst[:, bv_, :]), reads=[PB[bv_]], writes=[vtok.T[par]])

        def swa_scores(i, hs):
            bs = ps2()
            p0 = hs * 64
            for fc in range(4):
                for blk in range(2):
                    kcol = (i + blk) * 128
                    S.op("pe", lambda e, fc=fc, blk=blk, kcol=kcol: e.matmul(
                        pst[:, bs + fc // 2, ((fc % 2) * 2 + blk) * 128:((fc % 2) * 2 + blk + 1) * 128],
                        akT.t[p0:p0 + 64, kcol:kcol + 128], aq[p0:p0 + 64, fc, i * 128:(i + 1) * 128],
                        start=True, stop=True),
                        reads=akT.T + [fmA.T[0]], writes=[PB[bs + fc // 2]], signal=(fc % 2 == 1 and blk == 1))
            src = pst[:, bs:bs + 2, :].rearrange("p a (f b n) -> p (a f) b n", f=2, b=2)
            dst = PT.t[:, 4 * hs:4 * hs + 4]
            srcf = pst[:, bs:bs + 2, :]
            dstf = PT.t[:, 4 * hs:4 * hs + 4].rearrange("p f b n -> p (f b) n").rearrange("p (a x) n -> p a (x n)", a=2)
            if g == 0 and i == 0:
                S.op("act", lambda e: e.activation(dst[:, :, 0, :], src[:, :, 0, :], AF.Exp, bias=hbias.t[:, 0:1], scale=0.125),
                     reads=[PB[bs], PB[bs + 1]] + hbias.T, writes=[PT.T[hs]])
                S.op("act", lambda e: e.activation(dst[:, :, 1, :], src[:, :, 1, :], AF.Exp, scale=0.125),
                     reads=[PB[bs], PB[bs + 1]], writes=[PT.T[hs]])
            else:
                S.op("act", lambda e: e.activation(dstf, srcf, AF.Exp, scale=0.125),
                     reads=[PB[bs], PB[bs + 1]], writes=[PT.T[hs]])

        def swa_pv(i, vs, vsp):
            rd3 = rd.t[:].rearrange("p (f n) -> p f n", f=4)
            for hs in range(2):
                bank = ps1()
                for fc in range(4):
                    h = 4 * hs + fc
                    for qc in range(2):
                        q0 = qc * 64
                        if qc == 0:
                            segs = [(vsp, 0, 0, 128), (vs, 1, 0, 64)]
                        else:
                            segs = [(vs, 1, 0, 128), (vsp, 0, 64, 128)]
                        for si, (slot, blk, k0, k1) in enumerate(segs):
                            S.op("pe", lambda e, slot=slot, blk=blk, k0=k0, k1=k1, si=si, h=h, q0=q0, fc=fc: e.matmul(
                                pst[:, bank, fc * 128 + q0:fc * 128 + q0 + 64], vbuf.t[k0:k1, slot, hs, :],
                                PT.t[k0:k1, h, blk, q0:q0 + 64], start=(si == 0), stop=(si == 1)),
                                reads=[vbuf.T[slot], PT.T[hs]], writes=[PB[bank]],
                                signal=(si == 1 and fc == 3 and qc == 1))
                p3 = pst[:, bank, :].rearrange("p (f n) -> p f n", f=4)
                lo, hi = (0, 64) if hs == 0 else (64, 128)
                dl, dh = (64, 128) if hs == 0 else (0, 64)
                esb = bass.AP(esink.t, lo * 4, [[4, 64], [1, 4], [0, 128]])
                S.op("dve", lambda e: e.tensor_tensor(rd3[lo:hi], p3[dl:dh], esb, ALU.add),
                     reads=[PB[bank]] + esink.T, writes=[rdh[hs]])
                S.op("act", lambda e: e.activation(rd3[lo:hi], rd3[lo:hi], AF.Ln), reads=[rdh[hs]], writes=[rdh[hs]])
                S.op("act", lambda e: e.activation(rd3[lo:hi], rd3[lo:hi], AF.Exp, scale=-1.0),
                     reads=[rdh[hs]], writes=[rdh[hs]])
                S.op("dve", lambda e: e.tensor_tensor(rd3[lo:hi], p3[lo:hi], rd3[lo:hi], ALU.mult),
                     reads=[PB[bank], rdh[hs]], writes=[rdh[hs]])
                S.op("pool", lambda e: e.tensor_tensor(fmY.t[lo:hi, 0:4, i * 128:(i + 1) * 128], rd3[lo:hi],
                                                       sag[lo:hi, :, i * 128:(i + 1) * 128], ALU.mult),
                     reads=[rdh[hs], fmA.T[1]], writes=fmY.T[0:4])

        def stageB(i):
            gt = t0 + i
            par = gt % 2
            vs = gt % 5
            vsp = (gt - 1) % 5
            qk_ = qkr.t[:, par]
            kd_ = kd.t[:, par]
            vt_ = vtok.t[:, par]
            bt = ps1()
            pb = psb16(bt)
            for h8 in range(8):
                S.op("pe", lambda e, h8=h8: e.transpose(pb[:, h8 * 128:(h8 + 1) * 128], qk_[:, h8 * 128:(h8 + 1) * 128],
                                                        ident.t[:]),
                     reads=[qkr.T[par]] + ident.T, writes=[PB[bt]], signal=(h8 == 7))
            S.op("dve", lambda e: e.tensor_copy(qkT.t[:].rearrange("p h n -> p (h n)"), pb[:, :]),
                 reads=[PB[bt]], writes=qkT.T)
            S.op("dve", lambda e: e.tensor_tensor(qdT.t[:].rearrange("p h n -> p (h n)"), pb[:, 0:512], qdtab.t[:], ALU.mult),
                 reads=[PB[bt]] + qdtab.T, writes=qdT.T)
            swa_scores(i, 0)
            yield
            bsc = ps1()
            for hh in range(4):
                S.op("pe", lambda e, hh=hh: e.matmul(pst[:, bsc, hh * 128:(hh + 1) * 128], qkT.t[:, 4 + hh, :],
                                                     qkT.t[:, hh, :], start=True, stop=True),
                     reads=qkT.T, writes=[PB[bsc]], signal=(hh == 3))
            S.op("dve", lambda e: e.tensor_tensor(scT.t[:], pst[:, bsc, :], dtab.t[:], ALU.mult),
                 reads=[PB[bsc]] + dtab.T, writes=scT.T)
            bkv = ps1()
            reserved.add(bkv)
            for hh in range(4):
                S.op("pe", lambda e, hh=hh: e.matmul(pst[:, bkv, hh * 128:(hh + 1) * 128], kd_[:, hh * 128:(hh + 1) * 128],
                                                     vt_[:, hh * 128:(hh + 1) * 128], start=True, stop=True),
                     reads=[kd.T[par], vtok.T[par]], writes=[PB[bkv]], signal=(hh == 3))
            swa_scores(i, 1)
            yield
            bo = ps1()
            for hh in range(4):
                S.op("pe", lambda e, hh=hh: e.matmul(pst[:, bo, hh * 128:(hh + 1) * 128], scT.t[:, hh * 128:(hh + 1) * 128],
                                                     vt_[:, hh * 128:(hh + 1) * 128], start=True, stop=False),
                     reads=scT.T + [vtok.T[par]], writes=[PB[bo]], signal=False)
                S.op("pe", lambda e, hh=hh: e.matmul(pst[:, bo, hh * 128:(hh + 1) * 128], qdT.t[:, hh, :],
                                                     stb.t[:, hh * 128:(hh + 1) * 128], start=False, stop=True),
                     reads=qdT.T + stb.T, writes=[PB[bo]], signal=(hh == 3))
            for hh in range(4):
                S.op("dve", lambda e, hh=hh: e.scalar_tensor_tensor(
                    stf.t[:, hh * 128:(hh + 1) * 128], stf.t[:, hh * 128:(hh + 1) * 128], float(GAMMA[hh] ** 128),
                    pst[:, bkv, hh * 128:(hh + 1) * 128], ALU.mult, ALU.add),
                    reads=[PB[bkv]] + stf.T, writes=stf.T)
            reserved.discard(bkv)
            S.op("act", lambda e: e.copy(stb.t[:], stf.t[:]), reads=stf.T, writes=stb.T)
            nsidx[0] += 1
            c = (nsidx[0] % 4) * 4
            Tst = [stat.T[nsidx[0] % 4]]
            for hh in range(4):
                S.op("act", lambda e, hh=hh: e.activation(ybn.t[:, hh * 128:(hh + 1) * 128], pst[:, bo, hh * 128:(hh + 1) * 128],
                                                          AF.Square, accum_out=stat.t[:, c + hh:c + hh + 1]),
                     reads=[PB[bo]], writes=ybn.T + Tst)
            S.op("act", lambda e: e.activation(stat.t[:, c:c + 4], stat.t[:, c:c + 4], AF.Ln, bias=EPS, scale=1.0 / 128),
                 reads=Tst, writes=Tst)
            S.op("act", lambda e: e.activation(stat.t[:, c:c + 4], stat.t[:, c:c + 4], AF.Exp, scale=-0.5),
                 reads=Tst, writes=Tst)
            rb = bass.AP(stat.t, c, [[24, 128], [1, 4], [0, 128]])
            S.op("dve", lambda e: e.tensor_tensor(ybn.t[:].rearrange("p (h d) -> p h d", h=4),
                                                  pst[:, bo, :].rearrange("p (h d) -> p h d", h=4), rb, ALU.mult),
                 reads=[PB[bo]] + Tst, writes=ybn.T)
            swa_pv(i, vs, vsp)
            yield
            bt2 = ps1()
            pb2 = psb16(bt2)
            for hh in range(4):
                S.op("pe", lambda e, hh=hh: e.transpose(pb2[:, hh * 128:(hh + 1) * 128], ybn.t[:, hh * 128:(hh + 1) * 128],
                                                        ident.t[:]),
                     reads=ybn.T + ident.T, writes=[PB[bt2]], signal=(hh == 3))
            S.op("dve", lambda e: e.tensor_tensor(fmY.t[:, 4:8, i * 128:(i + 1) * 128],
                                                  pb2[:, 0:512].rearrange("p (h n) -> p h n", h=4),
                                                  sbg.t[:, :, i * 128:(i + 1) * 128], ALU.mult),
                 reads=[PB[bt2]] + sbg.T, writes=fmY.T[4:8])
            yield
            out_proj_tile(s_o0, s_o1, i, fmY)

        l0_post = make_prenorm(nt) if 1 in phases else None

        def run(gen):
            try:
                next(gen)
                return True
            except StopIteration:
                return False

        for _ in stageA(0):
            pass
        s_ag = load_block(1)
        for fc in range(4):
            proj_fm(s_ag, fc * 128, N, lambda p, Tb, fc=fc: S.op(
                "act", lambda e: e.activation(sag[:, fc, 0:N], p, AF.Silu), reads=Tb, writes=[fmA.T[1]]))
        release(s_ag)
        s_bg = load_block(2)
        for fc in range(4):
            proj_fm(s_bg, fc * 128, N, lambda p, Tb, fc=fc: S.op(
                "act", lambda e: e.activation(sbg.t[:, fc, 0:N], p, AF.Silu), reads=Tb, writes=sbg.T))
        release(s_bg)
        s_o0 = load_block(7)
        s_o1 = load_block(8)
        for i in range(nt):
            gB = stageB(i)
            gA = stageA(i + 1) if i + 1 < nt else None
            run(gB)
            if gA:
                run(gA)
            run(gB)
            if l0_post is not None and i > 0:
                l0_post(i - 1)
            run(gB)
            run(gB)
            if gA:
                run(gA)
                run(gA)
                run(gA)
            else:
                release(s_q, s_k, s_v)
            run(gB)
            run(gB)
        release(s_o0, s_o1)
        if l0_post is not None:
            l0_post(nt - 1)
        S.op("pool", lambda e: e.tensor_copy(akT.t[:, 0:128], akT.t[:, N:N + 128]), reads=akT.T, writes=akT.T)

    def phase_xattn(l, nt, post=None):
        N = nt * 128
        norm_group(nt)
        qx = fmA.t
        for j in range(2):
            s = load_block((9 if l == 0 else 23) + j)
            for fc in range(4):
                proj_fm(s, fc * 128, N, lambda p, Tb, fc=fc: S.op(
                    "act", lambda e: e.copy(qx[:, 4 * j + fc, 0:N], p), reads=Tb, writes=[fmA.T[j]]))
            release(s)
        sbanks = {}

        def scores(hx):
            bs = ps2()
            sbanks[hx] = bs
            for mc in range(2):
                for dc in range(2):
                    c = 2 * hx + dc
                    S.op("pe", lambda e, mc=mc, dc=dc, c=c: e.matmul(pst[:, bs + mc, 0:N], KmT.t[:, l, c, mc * 128:(mc + 1) * 128],
                                                                     qx[:, c, 0:N], start=(dc == 0), stop=(dc == 1)),
                         reads=[KmT.T[l], fmA.T[c // 4]], writes=[PB[bs + mc]], signal=(dc == 1))
            S.op("act", lambda e: e.activation(PT2.t[:, hx % 2, :, 0:N], pst[:, bs:bs + 2, 0:N], AF.Exp, scale=1.0 / 16),
                 reads=[PB[bs], PB[bs + 1]], writes=[PT2.T[hx % 2]])

        def pv(hx):
            bn0 = ps1()
            bn1 = ps1()
            bd = ps1()
            for mc in range(2):
                S.op("pe", lambda e, mc=mc: e.matmul(pst[:, bd, 0:N], ones.t[:, 2, :], PT2.t[:, hx % 2, mc, 0:N],
                                                     start=(mc == 0), stop=(mc == 1)),
                     reads=ones.T + [PT2.T[hx % 2]], writes=[PB[bd]], signal=(mc == 1))
            for dc, bn in enumerate((bn0, bn1)):
                for mc in range(2):
                    S.op("pe", lambda e, dc=dc, mc=mc, bn=bn: e.matmul(
                        pst[:, bn, 0:N], Vm.t[:, l, mc, (2 * hx + dc) * 128:(2 * hx + dc + 1) * 128],
                        PT2.t[:, hx % 2, mc, 0:N], start=(mc == 0), stop=(mc == 1)),
                        reads=[Vm.T[l], PT2.T[hx % 2]], writes=[PB[bn]], signal=(mc == 1))
            hp = hx % 2
            S.op("dve", lambda e: e.reciprocal(rdx[:, hp, 0:N], pst[:, bd, 0:N]), reads=[PB[bd]], writes=[rdx_T[hp]])
            if hx == 3:
                for dc, bn in enumerate((bn0, bn1)):
                    S.op("dve", lambda e, dc=dc, bn=bn: e.tensor_tensor(fmY.t[:, 2 * hx + dc, 0:N], pst[:, bn, 0:N],
                                                                        rdx[:, hp, 0:N], ALU.mult),
                         reads=[PB[bn], rdx_T[hp]], writes=[fmY.T[2 * hx + dc]])
                return
            for dc, bn in enumerate((bn0, bn1)):
                S.op("act", lambda e, dc=dc, bn=bn: e.copy(numsb[:, hp, dc, 0:N], pst[:, bn, 0:N]),
                     reads=[PB[bn]], writes=[numsb_T[hp]])
            for dc in range(2):
                S.op("pool", lambda e, dc=dc: e.tensor_tensor(fmY.t[:, 2 * hx + dc, 0:N], numsb[:, hp, dc, 0:N],
                                                               rdx[:, hp, 0:N], ALU.mult),
                     reads=[numsb_T[hp], rdx_T[hp]], writes=[fmY.T[2 * hx + dc]])

        scores(0)
        for hx in range(4):
            if hx + 1 < 4:
                scores(hx + 1)
            pv(hx)
        base = 11 if l == 0 else 25
        s0 = load_block(base)
        s1 = load_block(base + 1)
        out_proj(s0, s1, nt, fmY, list(range(nt)), post)
        release(s0, s1)

    def phase_l1(nt):
        N = nt * 128
        norm_group(nt)
        for j in range(4):
            sa = load_block(13 + 2 * j)
            sb_ = load_block(14 + 2 * j)
            for f2 in range(2):
                fc = 2 * j + f2
                k = fc % 2
                Tz = [zbuf.T[fc]]
                proj_fm(sa, f2 * 128, N, lambda p, Tb: S.op(
                    "act", lambda e: e.copy(gcs.t[:, k, 0:N], p), reads=Tb, writes=[gcs.T[k]]))
                proj_fm(sa, 256 + f2 * 128, N, lambda p, Tb: S.op(
                    "dve", lambda e: e.tensor_tensor(zbuf.t[:, fc, 2:2 + N], p, gcs.t[:, k, 0:N], ALU.mult),
                    reads=Tb + [gcs.T[k]], writes=Tz))
                S.op("pool", lambda e: e.tensor_scalar(cvt.t[:, k, 0:N], zbuf.t[:, fc, 2:2 + N], cw.t[:, 2, fc:fc + 1],
                                                       cw.t[:, 3, fc:fc + 1], ALU.mult, ALU.add),
                     reads=Tz + cw.T, writes=[cvt.T[k]])
                S.op("dve", lambda e: e.scalar_tensor_tensor(cvt.t[:, k, 0:N], zbuf.t[:, fc, 1:1 + N], cw.t[:, 1, fc:fc + 1],
                                                              cvt.t[:, k, 0:N], ALU.mult, ALU.add),
                     reads=Tz + cw.T + [cvt.T[k]], writes=[cvt.T[k]])
                S.op("dve", lambda e: e.scalar_tensor_tensor(cvt.t[:, k, 0:N], zbuf.t[:, fc, 0:N], cw.t[:, 0, fc:fc + 1],
                                                              cvt.t[:, k, 0:N], ALU.mult, ALU.add),
                     reads=Tz + cw.T + [cvt.T[k]], writes=[cvt.T[k]])
                S.op("pool", lambda e: e.tensor_copy(zbuf.t[:, fc, 0:2], zbuf.t[:, fc, N:N + 2]), reads=Tz, writes=Tz)
                proj_fm(sb_, f2 * 128, N, lambda p, Tb: S.op(
                    "dve", lambda e: e.tensor_tensor(cvt.t[:, k, 0:N], p, cvt.t[:, k, 0:N], ALU.mult),
                    reads=Tb + [cvt.T[k]], writes=[cvt.T[k]]))
                proj_fm(sb_, 256 + f2 * 128, N, lambda p, Tb: S.op(
                    "act", lambda e: e.activation(sgb.t[:, k, 0:N], p, AF.Silu), reads=Tb, writes=[sgb.T[k]]))
                S.op("pool", lambda e: e.tensor_tensor(fmY.t[:, fc, 0:N], cvt.t[:, k, 0:N], sgb.t[:, k, 0:N], ALU.mult),
                     reads=[cvt.T[k], sgb.T[k]], writes=[fmY.T[fc]])
            release(sa, sb_)
        s0 = load_block(21)
        s1 = load_block(22)
        out_proj(s0, s1, nt, fmY, list(range(nt)), make_prenorm(nt) if 3 in phases else None)
        release(s0, s1)

    maps = [{i: i for i in range(groups[0][1])}]
    for gi in range(1, len(groups)):
        prev = maps[gi - 1]
        free_now = [b for b in range(5) if b not in prev.values()]
        order = free_now + [prev[j] for j in sorted(prev)]
        maps.append({i: order[i] for i in range(groups[gi][1])})

    def x_load(gi, i):
        tt0 = groups[gi][0]
        ap_, t_ = XB[maps[gi][i]]
        S.dma(cq, ap_, xm.ap()[(tt0 + i) * 128:(tt0 + i + 1) * 128, :], writes=[t_])

    for i in range(groups[0][1]):
        x_load(0, i)
    for g, (t0, nt) in enumerate(groups):
        xmap.clear()
        xmap.update(maps[g])
        if g + 1 < len(groups):
            for i, b in maps[g + 1].items():
                if b not in maps[g].values():
                    x_load(g + 1, i)

        def finish_tile(i, g=g, t0=t0, nt=nt):
            gt = t0 + i
            if final:
                k = gt % 2
                c = 16 + 4 * k
                Tst = [stat.T[4 + k]]
                S.op("act", lambda e: e.activation(obuf.t[:, k], xa(i), AF.Square, accum_out=stat.t[:, c:c + 1]),
                     reads=[xT(i)], writes=[obuf.T[k]] + Tst)
                S.op("act", lambda e: e.activation(stat.t[:, c + 1:c + 2], stat.t[:, c:c + 1], AF.Ln, bias=EPS, scale=1.0 / D),
                     reads=Tst, writes=Tst)
                S.op("act", lambda e: e.activation(stat.t[:, c + 2:c + 3], stat.t[:, c + 1:c + 2], AF.Exp, scale=-0.5),
                     reads=Tst, writes=Tst)
                S.op("dve", lambda e: e.scalar_tensor_tensor(obuf.t[:, k], xa(i), stat.t[:, c + 2:c + 3], gfin.t[:],
                                                             ALU.mult, ALU.mult),
                     reads=[xT(i)] + Tst + gfin.T, writes=[obuf.T[k]])
                out_ops.append(S.dma(cq, out_d.ap()[gt * 128:(gt + 1) * 128, :], obuf.t[:, k], reads=[obuf.T[k]]))
            else:
                out_ops.append(S.dma(cq, out_d.ap()[gt * 128:(gt + 1) * 128, :], xa(i), reads=[xT(i)]))
            if g + 1 < len(groups):
                for j, b in maps[g + 1].items():
                    if b == maps[g][i]:
                        x_load(g + 1, j)
                if 0 in phases and nt == 4 and groups[g + 1][1] == 4:
                    if i >= 2:
                        early_b(i - 2)
                    if i >= 1:
                        ap_, t_ = XB[maps[g + 1][i - 1]]
                        early_a(i - 1, ap_, t_)

        last = max(phases)
        S.label = 'g%d_l0' % g
        if 0 in phases:
            phase_l0(g, t0, nt)
        S.label = 'g%d_xa0' % g
        if 1 in phases:
            phase_xattn(0, nt, finish_tile if last == 1 else make_prenorm(nt))
        S.label = 'g%d_l1' % g
        if 2 in phases:
            phase_l1(nt)
        S.label = 'g%d_xa1' % g
        if 3 in phases:
            phase_xattn(1, nt, finish_tile if last == 3 else None)
        if last in (0, 2):
            for i in range(nt):
                finish_tile(i)
    for d in out_ops:
        S._wait("sp", d)
        S._wait("pool", d)
    return nc, S


def _consts(half):
    start = 0 if half == 0 else (NT_PRE) * 128
    inv = 1.0 / (10000.0 ** np.linspace(0.0, 1.0, 64))

    def cstab(pos):
        ang = pos[:, None].astype(np.float64) * inv[None, :]
        c, s = np.cos(ang), np.sin(ang)
        return np.concatenate([c, s, -s], axis=1).astype(np.float32)

    csm = cstab(start + np.arange(NT_MAIN * 128)).reshape(NT_MAIN, 128, 192).transpose(1, 0, 2)
    csp = cstab(np.arange(NT_PRE * 128)).reshape(NT_PRE, 128, 192).transpose(1, 0, 2)
    sc = 128.0 ** -0.5
    i = np.arange(128)
    dt = np.zeros((128, 4, 128), np.float64)
    qd = np.zeros((128, 4, 128), np.float64)
    kdt = np.zeros((128, 4), np.float64)
    P = NT_PRE * 128
    kdp = np.zeros((128, NT_PRE, 4), np.float64)
    for h in range(4):
        g = GAMMA[h]
        ii, jj = i[None, :], i[:, None]
        same = (ii // 64) == (jj // 64)
        earlier = (jj // 64) < (ii // 64)
        m = np.where(same, g ** np.abs(ii - jj), np.where(earlier, g ** np.maximum(ii - jj, 0), 0.0))
        dt[:, h, :] = m * sc
        qd[:, h, :] = (g ** (i + 1.0))[None, :]
        kdt[:, h] = sc * g ** (127.0 - i)
        jg = (np.arange(NT_PRE)[None, :] * 128 + i[:, None]).astype(np.float64)
        kdp[:, :, h] = sc * g ** (P - 1.0 - jg)
    f = lambda a: np.ascontiguousarray(a.reshape(a.shape[0], -1) if a.ndim == 3 and a.shape[1] == 4 else a).astype(np.float32)
    return {
        "csm": np.ascontiguousarray(csm), "csp": np.ascontiguousarray(csp),
        "dtab": f(dt), "qdtab": f(qd), "kdtab": np.ascontiguousarray(kdt).astype(np.float32),
        "kdp": np.ascontiguousarray(kdp).astype(np.float32),
        "hbias": np.full((128, 1), -30000.0 if half == 0 else 0.0, np.float32),
        "ident": np.eye(128, dtype=np.float32),
    }


def make_in_maps(inp):
    f32 = lambda a: np.ascontiguousarray(np.asarray(a, dtype=np.float32))
    x = f32(inp["x"])
    mem = f32(inp["mem"])
    gl = [inp["e_norm"][0], inp["c_norm"][0], inp["o_norm"][0], inp["c_norm"][1],
          inp["c_mem_norm"][0], inp["c_mem_norm"][1]]
    gains = np.stack([f32(g).reshape(8, 128).T for g in gl], axis=1)
    gfin = np.ascontiguousarray(np.broadcast_to(f32(inp["final_norm"])[None, :], (128, D)))
    sk = f32(inp["e_sink"])[0]
    sink = np.zeros((128, 4), np.float32)
    for fc in range(4):
        sink[0:64, fc] = sk[fc]
        sink[64:128, fc] = sk[4 + fc]
    cwv = np.concatenate([f32(inp["o_conv_w"])[0], f32(inp["o_conv_b"])], axis=0)
    cw = np.ascontiguousarray(cwv.reshape(4, 8, 128).transpose(2, 0, 1))
    shared = {
        "e_w_in": f32(inp["e_w_in"][0]), "e_w_out": f32(inp["e_w_out"][0]),
        "o_w_in": f32(inp["o_w_in"][0]), "o_w_out": f32(inp["o_w_out"][0]),
        "c_wq0": f32(inp["c_wq"][0]), "c_wq1": f32(inp["c_wq"][1]),
        "c_wkv0": f32(inp["c_wkv"][0]), "c_wkv1": f32(inp["c_wkv"][1]),
        "c_wo0": f32(inp["c_wo"][0]), "c_wo1": f32(inp["c_wo"][1]),
        "gains": np.ascontiguousarray(gains), "gfin": gfin, "sink": sink, "cw": cw,
    }
    cons = [_consts(0), _consts(1)]
    maps = []
    for c in range(8):
        b, half = c // 2, c % 2
        m = dict(shared)
        m.update(cons[half])
        if half == 0:
            m["xm"] = np.ascontiguousarray(x[b, 0:NT_MAIN * 128])
            m["xp"] = np.zeros((NT_PRE * 128, D), np.float32)
        else:
            m["xm"] = np.ascontiguousarray(x[b, NT_PRE * 128:])
            m["xp"] = np.ascontiguousarray(x[b, 0:NT_PRE * 128])
        m["mem"] = np.ascontiguousarray(mem[b])
        maps.append(m)
    return maps


def assemble(results, B=4, SEQ=8192):
    out = np.empty((B, SEQ, D), np.float32)
    for c in range(8):
        b, half = c // 2, c % 2
        o = results[c]["out"]
        if half == 0:
            out[b, 0:4096] = o[0:4096]
        else:
            out[b, 4096:] = o[128:]
    return out


_CACHE = {}


def kernel(**inputs):
    if "nc" not in _CACHE:
        _CACHE["nc"] = build()[0]
    nc = _CACHE["nc"]
    maps = make_in_maps(inputs)
    res = run_bass_kernel_spmd(nc, maps, core_ids=list(range(8)))
    return assemble(res.results)
```

```python
import numpy as np
import concourse.bass as bass
import concourse.mybir as mybir
from concourse.bass_utils import run_bass_kernel_spmd

F32 = mybir.dt.float32
BF16 = mybir.dt.bfloat16
AF = mybir.ActivationFunctionType
ALU = mybir.AluOpType

NT_MAIN = 33
NT_PRE = 31
D = 1024
EPS = 1e-6
GROUPS = [(0, 1)] + [(1 + 4 * i, 4) for i in range(8)]
NSLOT = 6


class T:
    __slots__ = ("name", "writes", "reads", "sem", "cnt", "alias", "excl", "last_read")

    def __init__(self, name, excl=False):
        self.name = name
        self.excl = excl
        self.writes = []
        self.reads = []
        self.sem = None
        self.cnt = 0
        self.alias = []
        self.last_read = 0


class Op:
    __slots__ = ("eng", "sem", "val")

    def __init__(self, eng, sem, val):
        self.eng = eng
        self.sem = sem
        self.val = val


def _compress(lst):
    last = {}
    for o in lst:
        k = id(o.sem)
        if k not in last or last[k].val < o.val:
            last[k] = o
    return list(last.values())


class Sched:
    COMPUTE = ("pe", "act", "dve", "pool")

    def __init__(self, nc):
        self.nc = nc
        self.E = {"pe": nc.tensor, "act": nc.scalar, "dve": nc.vector,
                  "pool": nc.gpsimd, "sp": nc.sync}
        self.sem = {e: nc.alloc_semaphore("s_" + e) for e in self.COMPUTE}
        self.cnt = {e: 0 for e in self.COMPUTE}
        self.waited = {e: {} for e in self.E}
        self.nops = {e: 0 for e in self.E}
        self.nwaits = 0
        self.opidx = 0
        self.label = 'init'
        self.pe_labels = []

    def _wait(self, eng, d):
        w = self.waited[eng]
        key = id(d.sem)
        if w.get(key, 0) >= d.val:
            return
        w[key] = d.val
        self.E[eng].wait_ge(d.sem, d.val)
        self.nwaits += 1

    def _deps(self, eng, reads, writes, is_dma):
        deps = []
        for t in reads:
            for d in t.writes:
                if (not is_dma) and d.eng == eng and eng == "pe":
                    continue
                deps.append(d)
        for t in writes:
            for tt in [t] + t.alias:
                for d in tt.reads:
                    if (not is_dma) and d.eng == eng:
                        continue
                    deps.append(d)
                for d in tt.writes:
                    if (not is_dma) and d.eng == eng:
                        continue
                    deps.append(d)
        return deps

    def _record(self, op, reads, writes):
        self.opidx += 1
        for t in reads:
            t.last_read = self.opidx
            t.reads.append(op)
            if len(t.reads) > 48:
                t.reads = _compress(t.reads)
        for t in writes:
            for a in t.alias:
                a.reads = []
                a.writes = []
            if t.reads:
                t.writes = [op]
                t.reads = []
            else:
                t.writes.append(op)
                if len(t.writes) > 48:
                    t.writes = _compress(t.writes)

    def op(self, eng, fn, reads=(), writes=(), signal=True):
        ex = [t for t in reads if t.excl]
        if ex:
            reads = [t for t in reads if not t.excl]
            writes = list(writes) + ex
        for d in self._deps(eng, reads, writes, False):
            self._wait(eng, d)
        ins = fn(self.E[eng])
        self.nops[eng] += 1
        if eng == 'pe':
            self.pe_labels.append(self.label)
        if signal:
            self.cnt[eng] += 1
            ins.then_inc(self.sem[eng], 1)
            op = Op(eng, self.sem[eng], self.cnt[eng])
        else:
            op = Op(eng, self.sem[eng], self.cnt[eng] + 1)
        self._record(op, reads, writes)
        return op

    def dma(self, q, out_ap, in_ap, reads=(), writes=(), semt=None, **kw):
        for d in self._deps(q, reads, writes, True):
            self._wait(q, d)
        if semt is None:
            semt = writes[0] if writes else reads[0]
        if semt.sem is None:
            semt.sem = self.nc.alloc_semaphore("d_" + semt.name)
        ins = self.E[q].dma_start(out=out_ap, in_=in_ap, **kw)
        semt.cnt += 16
        ins.then_inc(semt.sem, 16)
        op = Op("dma", semt.sem, semt.cnt)
        self.nops[q] += 1
        self._record(op, reads, writes)
        return op


class Buf:
    def __init__(self, t, tiles):
        self.t = t
        self.T = tiles


def _blocks():
    B = []
    perm = []
    for fc in range(4):
        perm += [(fc * 64, 64), ((4 + fc) * 64, 64)]
    B.append(("e_w_in", 0, "e_norm", [(c0, n) for (c0, n) in perm]))
    B.append(("e_w_in", 0, "e_norm", [(768 + c0, n) for (c0, n) in perm]))
    B.append(("e_w_in", 0, "e_norm", [(2816, 512)]))
    B.append(("e_w_in", 0, "e_norm", [(512, 256), (512, 256)]))
    B.append(("e_w_in", 0, "e_norm", [(1280, 512)]))
    B.append(("e_w_in", 0, "e_norm", [(1792, 512)]))
    B.append(("e_w_in", 0, "e_norm", [(2304, 512)]))
    B.append(("e_w_out", 0, "PERM", [(0, 512)]))
    B.append(("e_w_out", 0, "PERM", [(512, 512)]))
    B.append(("c_wq", 0, "c_norm0", [(0, 512)]))
    B.append(("c_wq", 0, "c_norm0", [(512, 512)]))
    B.append(("c_wo", 0, None, [(0, 512)]))
    B.append(("c_wo", 0, None, [(512, 512)]))
    for j in range(4):
        B.append(("o_w_in", 0, "o_norm", [(1024 + 256 * j, 256), (2048 + 256 * j, 256)]))
        B.append(("o_w_in", 0, "o_norm", [(256 * j, 256), (3072 + 256 * j, 256)]))
    B.append(("o_w_out", 0, None, [(0, 512)]))
    B.append(("o_w_out", 0, None, [(512, 512)]))
    B.append(("c_wq", 1, "c_norm1", [(0, 512)]))
    B.append(("c_wq", 1, "c_norm1", [(512, 512)]))
    B.append(("c_wo", 1, None, [(0, 512)]))
    B.append(("c_wo", 1, None, [(512, 512)]))
    for l in range(2):
        for j in range(4):
            B.append(("c_wkv", l, "c_mem_norm%d" % l, [(512 * j, 512)]))
    return B


BLOCKS = _blocks()
NBLK_STREAM = 27
GAMMA = [1.0 - 2.0 ** (-5.0 - h) for h in range(4)]


def build(phases=(0, 1, 2, 3), final=True, groups=None, stage=9, nprep=None, dbg=None):
    dbg = dbg or {}
    nc = bass.Bass("TRN2", target_bir_lowering=False)
    S = Sched(nc)
    groups = GROUPS if groups is None else groups

    def din(name, shape, dt=F32):
        return nc.dram_tensor(name, list(shape), dt, kind="ExternalInput")

    xm = din("xm", [NT_MAIN * 128, D])
    xp = din("xp", [NT_PRE * 128, D])
    memd = din("mem", [256, D])
    W = {
        "e_w_in": [din("e_w_in", [D, 3328])],
        "e_w_out": [din("e_w_out", [D, D])],
        "o_w_in": [din("o_w_in", [D, 4096])],
        "o_w_out": [din("o_w_out", [D, D])],
        "c_wq": [din("c_wq0", [D, D]), din("c_wq1", [D, D])],
        "c_wkv": [din("c_wkv0", [D, 2048]), din("c_wkv1", [D, 2048])],
        "c_wo": [din("c_wo0", [D, D]), din("c_wo1", [D, D])],
    }
    gains_d = din("gains", [128, 6, 8])
    gfin_d = din("gfin", [128, D])
    sink_d = din("sink", [128, 4])
    cw_d = din("cw", [128, 4, 8])
    csm_d = din("csm", [128, NT_MAIN, 192])
    csp_d = din("csp", [128, NT_PRE, 192])
    dtab_d = din("dtab", [128, 512])
    qdtab_d = din("qdtab", [128, 512])
    kdtab_d = din("kdtab", [128, 4])
    kdp_d = din("kdp", [128, NT_PRE, 4])
    hbias_d = din("hbias", [128, 1])
    ident_d = din("ident", [128, 128])
    out_d = nc.dram_tensor("out", [NT_MAIN * 128, D], F32, kind="ExternalOutput")
    wsc = nc.dram_tensor("wsc", [len(BLOCKS), 128, 4096], BF16)
    GIDX = {"e_norm": 0, "c_norm0": 1, "o_norm": 2, "c_norm1": 3, "c_mem_norm0": 4, "c_mem_norm1": 5}

    def sb(name, shape, dt, ntiles=1):
        t = nc.alloc_sbuf_tensor("sb_" + name, list(shape), dt)
        return Buf(t, [T(name + str(i)) for i in range(ntiles)])

    xbuf = sb("xbuf", [128, 4, D], F32, 4)
    obuf = sb("obuf", [128, 2, D], F32, 2)
    ring = sb("ring", [128, NSLOT, 8, 512], BF16, NSLOT)
    xnT = sb("xnT", [128, 8, 512], BF16, 4)
    xs = sb("xs", [128, 2, D], BF16, 2)
    stat = sb("stat", [128, 24], F32, 6)
    fmA = sb("fmA", [128, 8, 512], BF16, 2)
    fmY = sb("fmY", [128, 8, 512], BF16, 8)
    sbg = sb("sbg", [128, 4, 512], BF16)
    akT = sb("akT", [128, 640], BF16)
    vbuf = sb("vbuf", [128, 5, 2, 128], BF16, 5)
    rtA = sb("rtA", [128, D], F32)
    rtB = sb("rtB", [128, D], F32)
    qkr = sb("qkr", [128, 2, D], BF16, 2)
    kd = sb("kd", [128, 2, 512], BF16, 2)
    vtok = sb("vtok", [128, 2, 512], BF16, 2)
    qkT = sb("qkT", [128, 8, 128], BF16)
    qdT = sb("qdT", [128, 4, 128], BF16)
    scT = sb("scT", [128, 512], BF16)
    stf = sb("stf", [128, 512], F32)
    stb = sb("stb", [128, 512], BF16)
    ybn = sb("ybn", [128, 512], BF16)
    PT = sb("PT", [128, 8, 2, 128], BF16, 2)
    rd = sb("rd", [128, 512], F32)
    PT2 = Buf(PT.t[:].rearrange("p h b n -> p (h b n)").rearrange("p (a m n) -> p a m n", a=2, m=2), PT.T)
    KmT = sb("KmT", [128, 2, 8, 256], BF16, 2)
    Vm = sb("Vm", [128, 2, 2, D], BF16, 2)
    zbuf = sb("zbuf", [128, 8, 516], F32, 8)
    gcs = sb("gcs", [128, 2, 512], F32, 2)
    cvt = sb("cvt", [128, 2, 512], F32, 2)
    sgb = sb("sgb", [128, 2, 512], F32, 2)
    gains = sb("gains", [128, 6, 8], F32)
    gfin = sb("gfin", [128, D], F32)
    esink = sb("esink", [128, 4], F32)
    cw = sb("cw", [128, 4, 8], F32)
    cs = sb("cs", [128, 4, 192], F32, 4)
    dtab = sb("dtab", [128, 512], F32)
    qdtab = sb("qdtab", [128, 512], F32)
    kdtab = sb("kdtab", [128, 4], F32)
    kdp = sb("kdp", [128, NT_PRE, 4], F32, 1)
    hbias = sb("hbias", [128, 1], F32)
    identf = Buf(rd.t[:, 0:128], rd.T)
    ident = sb("ident", [128, 128], BF16)
    ones = sb("ones", [128, 3, 128], BF16)
    xpb = sb("xpb", [128, 3, D], F32, 3)

    numsb = xpb.t[:, 0].bitcast(BF16).rearrange("p (a d n) -> p a d n", a=2, d=2)
    rdx = xpb.t[:, 1].rearrange("p (a n) -> p a n", a=2)
    numsb_T = [T("numsb0"), T("numsb1")]
    rdx_T = [T("rdx0"), T("rdx1")]
    for tt in numsb_T:
        tt.alias = [xpb.T[0]]
    for tt in rdx_T:
        tt.alias = [xpb.T[1]]
    xpb.T[0].alias = list(numsb_T)
    xpb.T[1].alias = list(rdx_T)

    rdh = [T("rdh0"), T("rdh1")]
    for tt in rdh:
        tt.alias = [rd.T[0]]
    rd.T[0].alias = list(rdh)
    x5T = T("x5")
    x5T.alias = [xpb.T[2]]
    xpb.T[2].alias = [x5T]
    XB = [(xbuf.t[:, j], xbuf.T[j]) for j in range(4)] + [(xpb.t[:, 2], x5T)]
    xmap = {}

    def xa(i):
        return XB[xmap[i]][0]

    def xT(i):
        return XB[xmap[i]][1]

    pst = nc.alloc_psum_tensor("ps", [128, 8, 512], F32)
    PB = [T("psb%d" % i, excl=True) for i in range(8)]
    prr = [0]
    reserved = set()

    def ps1():
        while True:
            i = prr[0] % 8
            prr[0] += 1
            if i not in reserved:
                return i

    def ps2():
        while True:
            if prr[0] % 2:
                prr[0] += 1
            i = prr[0] % 8
            prr[0] += 2
            if i not in reserved and (i + 1) not in reserved:
                return i

    def psb16(i):
        return pst[:, i, :].bitcast(BF16)

    def bcast(buf, off, dims):
        t = buf.t
        prow = int(np.prod(t.shape[1:]))
        return bass.AP(t, off, [[prow, 128]] + [[s, c] for (s, c) in dims])

    cq = "pool"
    S.dma(cq, gains.t[:], gains_d.ap(), writes=gains.T)
    S.dma(cq, gfin.t[:], gfin_d.ap(), writes=gfin.T)
    S.dma(cq, esink.t[:], sink_d.ap(), writes=esink.T)
    S.dma(cq, cw.t[:], cw_d.ap(), writes=cw.T)
    S.dma(cq, dtab.t[:], dtab_d.ap(), writes=dtab.T)
    S.dma(cq, qdtab.t[:], qdtab_d.ap(), writes=qdtab.T)
    S.dma(cq, kdtab.t[:], kdtab_d.ap(), writes=kdtab.T)
    S.dma(cq, hbias.t[:], hbias_d.ap(), writes=hbias.T)
    S.dma(cq, kdp.t[:], kdp_d.ap(), writes=kdp.T)
    S.dma(cq, identf.t, ident_d.ap(), writes=identf.T)
    S.op("dve", lambda e: e.tensor_copy(ident.t[:], identf.t), reads=identf.T, writes=ident.T)
    S.op("act", lambda e: e.activation(esink.t[:], esink.t[:], AF.Exp), reads=esink.T, writes=esink.T)
    S.op("pool", lambda e: e.memset(ones.t[:], 0.0), writes=ones.T)
    S.op("pool", lambda e: e.memset(ones.t[:, 0, 0:64], 1.0), writes=ones.T)
    S.op("pool", lambda e: e.memset(ones.t[:, 1, 64:128], 1.0), writes=ones.T)
    S.op("pool", lambda e: e.memset(ones.t[:, 2, :], 1.0), writes=ones.T)
    S.op("pool", lambda e: e.memset(vbuf.t[:], 1.0), writes=vbuf.T)
    S.op("dve", lambda e: e.memset(akT.t[:], 0.0), writes=akT.T)

    wsc_T = [T("wsc%d" % b) for b in range(len(BLOCKS))]
    STG = [(xbuf.t[:, 0:4].rearrange("p a (kc n) -> p (a kc) n", kc=2), xbuf.T),
           (zbuf.t[:, :, 0:512], zbuf.T)]
    OST = [(obuf.t[:].rearrange("p a n -> p (a n)").bitcast(BF16), obuf.T),
           (fmA.t[:].rearrange("p c n -> p (c n)"), fmA.T)]
    prep_cnt = [0]

    def prep_block(b):
        k = prep_cnt[0]
        prep_cnt[0] += 1
        wname, l, gname, segs = BLOCKS[b]
        wd = W[wname][l]
        stt, Ts = STG[k % 2]
        ob, To = OST[k % 2]
        Ts = list(Ts)
        To = list(To)
        c = 0
        for (c0, n) in segs:
            if gname == "PERM":
                for kc in range(4):
                    for hf in range(2):
                        r0 = (kc + 4 * hf) * 64
                        S.dma("sp", stt[hf * 64:(hf + 1) * 64, kc, c:c + n],
                              wd.ap()[r0:r0 + 64, c0:c0 + n], writes=Ts, semt=Ts[0])
                src = wd.ap()[512:1024, c0:c0 + n].rearrange("(kc p) n -> p kc n", p=128)
                S.dma("sp", stt[:, 4:8, c:c + n], src, writes=Ts, semt=Ts[0])
            else:
                src = wd.ap()[:, c0:c0 + n].rearrange("(kc p) n -> p kc n", p=128)
                S.dma("sp", stt[:, :, c:c + n], src, writes=Ts, semt=Ts[0])
            c += n
        ob4 = ob.rearrange("p (k n) -> p k n", k=8)
        for kc in range(8):
            src = stt[:, kc, :]
            if gname in GIDX:
                gi = GIDX[gname]
                if kc % 2 == 0:
                    S.op("dve", lambda e, kc=kc, src=src: e.tensor_scalar_mul(ob4[:, kc, :], src, gains.t[:, gi, kc:kc + 1]),
                         reads=Ts + gains.T, writes=To)
                else:
                    S.op("act", lambda e, kc=kc, src=src: e.mul(ob4[:, kc, :], src, gains.t[:, gi, kc:kc + 1]),
                         reads=Ts + gains.T, writes=To)
            else:
                if kc % 2 == 0:
                    S.op("dve", lambda e, kc=kc, src=src: e.tensor_copy(ob4[:, kc, :], src), reads=Ts, writes=To)
                else:
                    S.op("act", lambda e, kc=kc, src=src: e.copy(ob4[:, kc, :], src), reads=Ts, writes=To)
        S.dma("act", wsc.ap()[b], ob, reads=To, writes=[wsc_T[b]], semt=To[0])

    prep_rest = list(range(27, 35)) + [b for b in range(27) if b not in (5, 6, 3)]
    for b in (5, 6, 3):
        prep_block(b)

    ring_pos = [0]

    pinned = set()

    def load_block(b):
        cands = [q for q in range(NSLOT) if q not in pinned]
        assert cands, "weight ring: all slots pinned"
        s = min(cands, key=lambda q: ring.T[q].last_read)
        pinned.add(s)
        S.dma("sp", ring.t[:, s].rearrange("p k n -> p (k n)"), wsc.ap()[b], reads=[wsc_T[b]],
              writes=[ring.T[s]])
        ring.T[s].last_read = S.opidx
        return s

    def release(*slots):
        for q in slots:
            pinned.discard(q)

    def wst(s, kc, c0, n=128):
        return ring.t[:, s, kc, c0:c0 + n]

    def wmv(s, kc):
        return ring.t[:, s, kc, :]

    def norm_a(src_ap, srcT, sidx):
        st = stat.t
        Tst = [stat.T[sidx % 4]]
        c = (sidx % 4) * 4
        k = sidx % 2
        S.op("act", lambda e: e.activation(xs.t[:, k], src_ap, AF.Square, accum_out=st[:, c:c + 1]),
             reads=srcT, writes=[xs.T[k]] + Tst)
        S.op("act", lambda e: e.activation(st[:, c + 1:c + 2], st[:, c:c + 1], AF.Ln, bias=EPS, scale=1.0 / D),
             reads=Tst, writes=Tst)
        S.op("act", lambda e: e.activation(st[:, c + 2:c + 3], st[:, c + 1:c + 2], AF.Exp, scale=-0.5),
             reads=Tst, writes=Tst)
        S.op("act", lambda e: e.mul(xs.t[:, k], src_ap, st[:, c + 2:c + 3]), reads=srcT + Tst, writes=[xs.T[k]])

    def norm_b(col0, sidx):
        k = sidx % 2
        b = ps1()
        pb = psb16(b)
        for kc in range(8):
            S.op("pe", lambda e, kc=kc: e.transpose(pb[:, kc * 128:(kc + 1) * 128], xs.t[:, k, kc * 128:(kc + 1) * 128],
                                                    ident.t[:]),
                 reads=[xs.T[k]] + ident.T, writes=[PB[b]], signal=(kc == 7))
        S.op("dve", lambda e: e.tensor_copy(xnT.t[:, :, col0:col0 + 128],
                                            pb[:, :].rearrange("p (kc n) -> p kc n", kc=8)),
             reads=[PB[b]], writes=[xnT.T[col0 // 128]])

    def rmsnorm_T(src_ap, srcT, col0, sidx):
        norm_a(src_ap, srcT, sidx)
        norm_b(col0, sidx)

    def proj_fm(s, c0, N, evac):
        b = ps1()
        for kc in range(8):
            S.op("pe", lambda e, kc=kc: e.matmul(pst[:, b, 0:N], wst(s, kc, c0), xnT.t[:, kc, 0:N],
                                                 start=(kc == 0), stop=(kc == 7)),
                 reads=[ring.T[s]] + xnT.T, writes=[PB[b]], signal=(kc == 7))
        evac(pst[:, b, 0:N], [PB[b]])

    def out_proj_tile(s0, s1, i, lhs_buf, post=None):
        b = ps2()
        for cb, s in enumerate((s0, s1)):
            for kc in range(8):
                S.op("pe", lambda e, kc=kc, cb=cb, s=s: e.matmul(
                    pst[:, b + cb, :],
                    lhs_buf.t[:, kc, i * 128:(i + 1) * 128], wmv(s, kc),
                    start=(kc == 0), stop=(kc == 7)),
                    reads=[ring.T[s], lhs_buf.T[kc]], writes=[PB[b + cb]], signal=(kc == 7))
        xi, xti = xa(i), xT(i)
        S.op("dve", lambda e: e.tensor_tensor(xi.rearrange("p (a n) -> p a n", a=2),
                                              pst[:, b:b + 2, :],
                                              xi.rearrange("p (a n) -> p a n", a=2), ALU.add),
             reads=[PB[b], PB[b + 1], xti], writes=[xti])
        if post is not None:
            post(i)

    def out_proj(s0, s1, nt, lhs_buf, xtiles, post=None):
        for i in range(nt):
            out_proj_tile(s0, s1, i, lhs_buf, post)

    def rotary(src3, nh, cst, cT, srcT):
        cs_off = cst
        cosb = bass.AP(cs.t, cs_off, [[4 * 192, 128], [0, nh * 2], [1, 64]])
        sinb = bass.AP(cs.t, cs_off + 64, [[4 * 192, 128], [0, nh], [1, 64]])
        nsinb = bass.AP(cs.t, cs_off + 128, [[4 * 192, 128], [0, nh], [1, 64]])
        A3 = rtA.t[:, 0:nh * 128].rearrange("p (h d) -> p h d", h=nh)
        B3 = rtB.t[:, 0:nh * 128].rearrange("p (h d) -> p h d", h=nh)
        src2 = src3.rearrange("p h (t d) -> p (h t) d", t=2)
        A2 = rtA.t[:, 0:nh * 128].rearrange("p (h d) -> p h d", h=nh * 2)
        S.op("dve", lambda e: e.tensor_tensor(A2, src2, cosb, ALU.mult), reads=srcT + cT, writes=rtA.T)
        S.op("dve", lambda e: e.tensor_tensor(B3[:, :, 0:64], src3[:, :, 64:128], nsinb, ALU.mult),
             reads=srcT + cT, writes=rtB.T)
        S.op("dve", lambda e: e.tensor_tensor(B3[:, :, 64:128], src3[:, :, 0:64], sinb, ALU.mult),
             reads=srcT + cT, writes=rtB.T)
        S.op("pool", lambda e: e.tensor_tensor(rtA.t[:, 0:nh * 128], rtA.t[:, 0:nh * 128], rtB.t[:, 0:nh * 128], ALU.add),
             reads=rtA.T + rtB.T, writes=rtA.T)

    def halo_kv(s3, col0, vslot):
        b = ps1()
        for kc in range(8):
            S.op("pe", lambda e, kc=kc: e.matmul(pst[:, b, 0:128], wst(s3, kc, 0), xnT.t[:, kc, col0:col0 + 128],
                                                 start=(kc == 0), stop=(kc == 7)),
                 reads=[ring.T[s3], xnT.T[col0 // 128]], writes=[PB[b]], signal=(kc == 7))
        S.op("act", lambda e: e.copy(akT.t[:, 0:128], pst[:, b, 0:128]), reads=[PB[b]], writes=akT.T)
        av_tok(s3, col0, vslot)

    def av_tok(s3, col0, vslot):
        b = ps1()
        for kc in range(8):
            S.op("pe", lambda e, kc=kc: e.matmul(pst[:, b, 0:128], xnT.t[:, kc, col0:col0 + 128], wst(s3, kc, 128),
                                                 start=(kc == 0), stop=(kc == 7)),
                 reads=[ring.T[s3], xnT.T[col0 // 128]], writes=[PB[b]], signal=(kc == 7))
        for kvh in range(2):
            S.op("act", lambda e, kvh=kvh: e.copy(vbuf.t[:, vslot, kvh, kvh * 64:(kvh + 1) * 64],
                                                  pst[:, b, kvh * 64:(kvh + 1) * 64]),
                 reads=[PB[b]], writes=[vbuf.T[vslot]])

    if 0 not in phases:
        while prep_rest:
            prep_block(prep_rest.pop(0))
    if 0 in phases:
        s_bk = load_block(5)
        s_bv = load_block(6)
        s_b3 = load_block(3)
        bst = ps1()
        reserved.add(bst)
        NPRE = dbg.get('npre', NT_PRE)
        pbanks = {}

        def pre_load(t):
            S.dma(cq, xpb.t[:, t % 3], xp.ap()[t * 128:(t + 1) * 128, :], writes=[xpb.T[t % 3]])
            S.dma(cq, cs.t[:, t % 4], csp_d.ap()[:, t, :], writes=[cs.T[t % 4]])

        def pre_stage(sg, t):
            k3 = t % 4
            k2 = t % 3
            if sg == 0:
                norm_a(xpb.t[:, k2], [xpb.T[k2]], t)
            elif sg == 1:
                norm_b(0, t)
            elif sg == 2:
                bk_ = ps1()
                reserved.add(bk_)
                bv_ = ps1()
                reserved.add(bv_)
                pbanks[t] = (bk_, bv_)
                for kc in range(8):
                    S.op("pe", lambda e, kc=kc: e.matmul(pst[:, bk_, :],
                                                         xnT.t[:, kc, 0:128], wmv(s_bk, kc), start=(kc == 0), stop=(kc == 7)),
                         reads=[ring.T[s_bk]] + xnT.T, writes=[PB[bk_]], signal=(kc == 7))
                for kc in range(8):
                    S.op("pe", lambda e, kc=kc: e.matmul(pst[:, bv_, :],
                                                         xnT.t[:, kc, 0:128], wmv(s_bv, kc), start=(kc == 0), stop=(kc == 7)),
                         reads=[ring.T[s_bv]] + xnT.T, writes=[PB[bv_]], signal=(kc == 7))
                if t == NPRE - 1:
                    halo_kv(s_b3, 0, (0 - 1) % 5)
            elif sg == 3:
                bk_, bv_ = pbanks.pop(t)
                rotary(pst[:, bk_, :].rearrange("p (h d) -> p h d", h=4), 4, k3 * 192, [cs.T[k3]], [PB[bk_]])
                kdb = bass.AP(kdp.t, t * 4, [[NT_PRE * 4, 128], [1, 4], [0, 128]])
                S.op("pool", lambda e: e.tensor_tensor(kd.t[:, 0].rearrange("p (h d) -> p h d", h=4),
                                                       rtA.t[:, 0:512].rearrange("p (h d) -> p h d", h=4), kdb, ALU.mult),
                     reads=rtA.T + kdp.T, writes=[kd.T[0]])
                S.op("act", lambda e: e.copy(vtok.t[:, 0], pst[:, bv_, :]), reads=[PB[bv_]], writes=[vtok.T[0]])
                reserved.discard(bk_)
                reserved.discard(bv_)
            else:
                for hh in range(4):
                    S.op("pe", lambda e, hh=hh: e.matmul(pst[:, bst, hh * 128:(hh + 1) * 128], kd.t[:, 0, hh * 128:(hh + 1) * 128],
                                                         vtok.t[:, 0, hh * 128:(hh + 1) * 128],
                                                         start=(t == 0 and hh == 0), stop=(t == NPRE - 1)),
                         reads=[kd.T[0], vtok.T[0]], writes=[PB[bst]], signal=(hh == 3))

        pre_load(0)
        for step in range(NPRE + 4):
            if prep_rest:
                prep_block(prep_rest.pop(0))
            for sg in (4, 3, 2, 1, 0):
                t = step - sg
                if 0 <= t < NPRE:
                    pre_stage(sg, t)
            if step + 1 < NPRE:
                pre_load(step + 1)
        S.op("dve", lambda e: e.tensor_copy(stf.t[:], pst[:, bst, :]), reads=[PB[bst]], writes=stf.T)
        S.op("act", lambda e: e.copy(stb.t[:], pst[:, bst, :]), reads=[PB[bst]], writes=stb.T)
        reserved.discard(bst)
        release(s_bk, s_bv, s_b3)

    while prep_rest:
        prep_block(prep_rest.pop(0))
    for mt in range(2):
        S.dma(cq, xpb.t[:, mt], memd.ap()[mt * 128:(mt + 1) * 128, :], writes=[xpb.T[mt]])
    for mt in range(2):
        rmsnorm_T(xpb.t[:, mt], [xpb.T[mt]], mt * 128, mt)
    memT = Buf(fmY.t[:, :, 0:256], fmY.T)
    S.op("dve", lambda e: e.tensor_copy(memT.t, xnT.t[:, :, 0:256]), reads=xnT.T, writes=memT.T)
    for l in range(2):
        for j in range(2):
            s = load_block(27 + 4 * l + j)
            for fc in range(4):
                b = ps1()
                for kc in range(8):
                    S.op("pe", lambda e, kc=kc, fc=fc: e.matmul(pst[:, b, 0:256], wst(s, kc, fc * 128), memT.t[:, kc, :],
                                                                start=(kc == 0), stop=(kc == 7)),
                         reads=[ring.T[s]] + memT.T, writes=[PB[b]], signal=(kc == 7))
                S.op("act", lambda e, fc=fc: e.copy(KmT.t[:, l, 4 * j + fc, :], pst[:, b, 0:256]),
                     reads=[PB[b]], writes=[KmT.T[l]])
            release(s)
        for j in range(2):
            s = load_block(27 + 4 * l + 2 + j)
            for mc in range(2):
                b = ps1()
                for kc in range(8):
                    S.op("pe", lambda e, kc=kc, mc=mc: e.matmul(pst[:, b, :],
                                                                memT.t[:, kc, mc * 128:(mc + 1) * 128], wmv(s, kc),
                                                                start=(kc == 0), stop=(kc == 7)),
                         reads=[ring.T[s]] + memT.T, writes=[PB[b]], signal=(kc == 7))
                S.op("act", lambda e, mc=mc: e.copy(Vm.t[:, l, mc, j * 512:(j + 1) * 512], pst[:, b, :]),
                     reads=[PB[b]], writes=[Vm.T[l]])
            release(s)

    S.op("pool", lambda e: e.memset(zbuf.t[:], 0.0), writes=zbuf.T)
    out_ops = []
    nsidx = [0]

    prenormed = [False]
    early = {"a": [], "b": [], "sid": {}}

    def early_a(j, ap_, t_):
        nsidx[0] += 1
        early["sid"][j] = nsidx[0]
        norm_a(ap_, [t_], nsidx[0])
        early["a"].append(j)

    def early_b(j):
        norm_b(j * 128, early["sid"][j])
        early["b"].append(j)

    def norm_group(nt):
        if prenormed[0]:
            prenormed[0] = False
            return
        for i in range(nt):
            if i not in early["a"]:
                early_a(i, xa(i), xT(i))
            if i >= 1 and (i - 1) not in early["b"]:
                early_b(i - 1)
        if (nt - 1) not in early["b"]:
            early_b(nt - 1)
        early["a"], early["b"], early["sid"] = [], [], {}

    def make_prenorm(nt):
        pend = []

        def post(i):
            if pend:
                j, sj = pend.pop()
                norm_b(j * 128, sj)
            nsidx[0] += 1
            norm_a(xa(i), [xT(i)], nsidx[0])
            pend.append((i, nsidx[0]))
            if i == nt - 1:
                j, sj = pend.pop()
                norm_b(j * 128, sj)
                prenormed[0] = True
        return post

    def phase_l0(g, t0, nt):
        N = nt * 128
        norm_group(nt)
        aq, sag = fmA.t[:, 0:4], fmA.t[:, 4:8]
        s_aq = load_block(0)
        s3 = load_block(3)
        s_q = load_block(4)
        s_k = load_block(5)
        s_v = load_block(6)
        for fc in range(4):
            proj_fm(s_aq, fc * 128, N, lambda p, Tb, fc=fc: S.op(
                "act", lambda e: e.copy(aq[:, fc, 0:N], p), reads=Tb, writes=[fmA.T[0]]))
        release(s_aq)
        proj_fm(s3, 0, N, lambda p, Tb: S.op(
            "act", lambda e: e.copy(akT.t[:, 128:128 + N], p), reads=Tb, writes=akT.T))
        for i in range(nt):
            av_tok(s3, i * 128, (t0 + i) % 5)
        release(s3)

        def tok_proj(sl, bank, i):
            for kc in range(8):
                S.op("pe", lambda e, kc=kc: e.matmul(pst[:, bank, :],
                                                     xnT.t[:, kc, i * 128:(i + 1) * 128], wmv(sl, kc),
                                                     start=(kc == 0), stop=(kc == 7)),
                     reads=[ring.T[sl], xnT.T[i]], writes=[PB[bank]], signal=(kc == 7))

        def stageA(i):
            gt = t0 + i
            par = gt % 2
            k3 = gt % 3
            S.dma(cq, cs.t[:, k3], csm_d.ap()[:, gt, :], writes=[cs.T[k3]])
            bq = ps2()
            reserved.update((bq, bq + 1))
            tok_proj(s_q, bq, i)
            yield
            tok_proj(s_k, bq + 1, i)
            yield
            bv_ = ps1()
            tok_proj(s_v, bv_, i)
            rotary(pst[:, bq:bq + 2, :].rearrange("p a (h d) -> p (a h) d", h=4), 8, k3 * 192, [cs.T[k3]],
                   [PB[bq], PB[bq + 1]])
            reserved.difference_update((bq, bq + 1))
            S.op("act", lambda e: e.copy(qkr.t[:, par], rtA.t[:]), reads=rtA.T, writes=[qkr.T[par]])
            kdb = bass.AP(kdtab.t, 0, [[4, 128], [1, 4], [0, 128]])
            S.op("pool", lambda e: e.tensor_tensor(kd.t[:, par].rearrange("p (h d) -> p h d", h=4),
                                                   rtA.t[:, 512:1024].rearrange("p (h d) -> p h d", h=4), kdb, ALU.mult),
                 reads=rtA.T + kdtab.T, writes=[kd.T[par]])
            S.op("act", lambda e: e.copy(vtok.t[:, par], pst[:, bv_, :]), reads=[PB[bv_]], writes=[vtok.T[par]])

        def swa_scores(i, hs):
            bs = ps2()
            p0 = hs * 64
            for fc in range(4):
                for blk in range(2):
                    kcol = (i + blk) * 128
                    S.op("pe", lambda e, fc=fc, blk=blk, kcol=kcol: e.matmul(
                        pst[:, bs + fc // 2, ((fc % 2) * 2 + blk) * 128:((fc % 2) * 2 + blk + 1) * 128],
                        akT.t[p0:p0 + 64, kcol:kcol + 128], aq[p0:p0 + 64, fc, i * 128:(i + 1) * 128],
                        start=True, stop=True),
                        reads=akT.T + [fmA.T[0]], writes=[PB[bs + fc // 2]], signal=(fc % 2 == 1 and blk == 1))
            src = pst[:, bs:bs + 2, :].rearrange("p a (f b n) -> p (a f) b n", f=2, b=2)
            dst = PT.t[:, 4 * hs:4 * hs + 4]
            srcf = pst[:, bs:bs + 2, :]
            dstf = PT.t[:, 4 * hs:4 * hs + 4].rearrange("p f b n -> p (f b) n").rearrange("p (a x) n -> p a (x n)", a=2)
            if g == 0 and i == 0:
                S.op("act", lambda e: e.activation(dst[:, :, 0, :], src[:, :, 0, :], AF.Exp, bias=hbias.t[:, 0:1], scale=0.125),
                     reads=[PB[bs], PB[bs + 1]] + hbias.T, writes=[PT.T[hs]])
                S.op("act", lambda e: e.activation(dst[:, :, 1, :], src[:, :, 1, :], AF.Exp, scale=0.125),
                     reads=[PB[bs], PB[bs + 1]], writes=[PT.T[hs]])
            else:
                S.op("act", lambda e: e.activation(dstf, srcf, AF.Exp, scale=0.125),
                     reads=[PB[bs], PB[bs + 1]], writes=[PT.T[hs]])

        def swa_pv(i, vs, vsp):
            rd3 = rd.t[:].rearrange("p (f n) -> p f n", f=4)
            for hs in range(2):
                bank = ps1()
                for fc in range(4):
                    h = 4 * hs + fc
                    for qc in range(2):
                        q0 = qc * 64
                        if qc == 0:
                            segs = [(vsp, 0, 0, 128), (vs, 1, 0, 64)]
                        else:
                            segs = [(vs, 1, 0, 128), (vsp, 0, 64, 128)]
                        for si, (slot, blk, k0, k1) in enumerate(segs):
                            S.op("pe", lambda e, slot=slot, blk=blk, k0=k0, k1=k1, si=si, h=h, q0=q0, fc=fc: e.matmul(
                                pst[:, bank, fc * 128 + q0:fc * 128 + q0 + 64], vbuf.t[k0:k1, slot, hs, :],
                                PT.t[k0:k1, h, blk, q0:q0 + 64], start=(si == 0), stop=(si == 1)),
                                reads=[vbuf.T[slot], PT.T[hs]], writes=[PB[bank]],
                                signal=(si == 1 and fc == 3 and qc == 1))
                p3 = pst[:, bank, :].rearrange("p (f n) -> p f n", f=4)
                lo, hi = (0, 64) if hs == 0 else (64, 128)
                dl, dh = (64, 128) if hs == 0 else (0, 64)
                esb = bass.AP(esink.t, lo * 4, [[4, 64], [1, 4], [0, 128]])
                S.op("dve", lambda e: e.tensor_tensor(rd3[lo:hi], p3[dl:dh], esb, ALU.add),
                     reads=[PB[bank]] + esink.T, writes=[rdh[hs]])
                S.op("act", lambda e: e.activation(rd3[lo:hi], rd3[lo:hi], AF.Ln), reads=[rdh[hs]], writes=[rdh[hs]])
                S.op("act", lambda e: e.activation(rd3[lo:hi], rd3[lo:hi], AF.Exp, scale=-1.0),
                     reads=[rdh[hs]], writes=[rdh[hs]])
                S.op("dve", lambda e: e.tensor_tensor(rd3[lo:hi], p3[lo:hi], rd3[lo:hi], ALU.mult),
                     reads=[PB[bank], rdh[hs]], writes=[rdh[hs]])
                S.op("pool", lambda e: e.tensor_tensor(fmY.t[lo:hi, 0:4, i * 128:(i + 1) * 128], rd3[lo:hi],
                                                       sag[lo:hi, :, i * 128:(i + 1) * 128], ALU.mult),
                     reads=[rdh[hs], fmA.T[1]], writes=fmY.T[0:4])

        def stageB(i):
            gt = t0 + i
            par = gt % 2
            vs = gt % 5
            vsp = (gt - 1) % 5
            qk_ = qkr.t[:, par]
            kd_ = kd.t[:, par]
            vt_ = vtok.t[:, par]
            bt = ps1()
            pb = psb16(bt)
            for h8 in range(8):
                S.op("pe", lambda e, h8=h8: e.transpose(pb[:, h8 * 128:(h8 + 1) * 128], qk_[:, h8 * 128:(h8 + 1) * 128],
                                                        ident.t[:]),
                     reads=[qkr.T[par]] + ident.T, writes=[PB[bt]], signal=(h8 == 7))
            S.op("dve", lambda e: e.tensor_copy(qkT.t[:].rearrange("p h n -> p (h n)"), pb[:, :]),
                 reads=[PB[bt]], writes=qkT.T)
            S.op("dve", lambda e: e.tensor_tensor(qdT.t[:].rearrange("p h n -> p (h n)"), pb[:, 0:512], qdtab.t[:], ALU.mult),
                 reads=[PB[bt]] + qdtab.T, writes=qdT.T)
            swa_scores(i, 0)
            yield
            bsc = ps1()
            for hh in range(4):
                S.op("pe", lambda e, hh=hh: e.matmul(pst[:, bsc, hh * 128:(hh + 1) * 128], qkT.t[:, 4 + hh, :],
                                                     qkT.t[:, hh, :], start=True, stop=True),
                     reads=qkT.T, writes=[PB[bsc]], signal=(hh == 3))
            S.op("dve", lambda e: e.tensor_tensor(scT.t[:], pst[:, bsc, :], dtab.t[:], ALU.mult),
                 reads=[PB[bsc]] + dtab.T, writes=scT.T)
            bkv = ps1()
            reserved.add(bkv)
            for hh in range(4):
                S.op("pe", lambda e, hh=hh: e.matmul(pst[:, bkv, hh * 128:(hh + 1) * 128], kd_[:, hh * 128:(hh + 1) * 128],
                                                     vt_[:, hh * 128:(hh + 1) * 128], start=True, stop=True),
                     reads=[kd.T[par], vtok.T[par]], writes=[PB[bkv]], signal=(hh == 3))
            swa_scores(i, 1)
            yield
            bo = ps1()
            for hh in range(4):
                S.op("pe", lambda e, hh=hh: e.matmul(pst[:, bo, hh * 128:(hh + 1) * 128], scT.t[:, hh * 128:(hh + 1) * 128],
                                                     vt_[:, hh * 128:(hh + 1) * 128], start=True, stop=False),
                     reads=scT.T + [vtok.T[par]], writes=[PB[bo]], signal=False)
                S.op("pe", lambda e, hh=hh: e.matmul(pst[:, bo, hh * 128:(hh + 1) * 128], qdT.t[:, hh, :],
                                                     stb.t[:, hh * 128:(hh + 1) * 128], start=False, stop=True),
                     reads=qdT.T + stb.T, writes=[PB[bo]], signal=(hh == 3))
            for hh in range(4):
                S.op("dve", lambda e, hh=hh: e.scalar_tensor_tensor(
                    stf.t[:, hh * 128:(hh + 1) * 128], stf.t[:, hh * 128:(hh + 1) * 128], float(GAMMA[hh] ** 128),
                    pst[:, bkv, hh * 128:(hh + 1) * 128], ALU.mult, ALU.add),
                    reads=[PB[bkv]] + stf.T, writes=stf.T)
            reserved.discard(bkv)
            S.op("act", lambda e: e.copy(stb.t[:], stf.t[:]), reads=stf.T, writes=stb.T)
            nsidx[0] += 1
            c = (nsidx[0] % 4) * 4
            Tst = [stat.T[nsidx[0] % 4]]
            for hh in range(4):
                S.op("act", lambda e, hh=hh: e.activation(ybn.t[:, hh * 128:(hh + 1) * 128], pst[:, bo, hh * 128:(hh + 1) * 128],
                                                          AF.Square, accum_out=stat.t[:, c + hh:c + hh + 1]),
                     reads=[PB[bo]], writes=ybn.T + Tst)
            S.op("act", lambda e: e.activation(stat.t[:, c:c + 4], stat.t[:, c:c + 4], AF.Ln, bias=EPS, scale=1.0 / 128),
                 reads=Tst, writes=Tst)
            S.op("act", lambda e: e.activation(stat.t[:, c:c + 4], stat.t[:, c:c + 4], AF.Exp, scale=-0.5),
                 reads=Tst, writes=Tst)
            rb = bass.AP(stat.t, c, [[24, 128], [1, 4], [0, 128]])
            S.op("dve", lambda e: e.tensor_tensor(ybn.t[:].rearrange("p (h d) -> p h d", h=4),
                                                  pst[:, bo, :].rearrange("p (h d) -> p h d", h=4), rb, ALU.mult),
                 reads=[PB[bo]] + Tst, writes=ybn.T)
            swa_pv(i, vs, vsp)
            yield
            bt2 = ps1()
            pb2 = psb16(bt2)
            for hh in range(4):
                S.op("pe", lambda e, hh=hh: e.transpose(pb2[:, hh * 128:(hh + 1) * 128], ybn.t[:, hh * 128:(hh + 1) * 128],
                                                        ident.t[:]),
                     reads=ybn.T + ident.T, writes=[PB[bt2]], signal=(hh == 3))
            S.op("dve", lambda e: e.tensor_tensor(fmY.t[:, 4:8, i * 128:(i + 1) * 128],
                                                  pb2[:, 0:512].rearrange("p (h n) -> p h n", h=4),
                                                  sbg.t[:, :, i * 128:(i + 1) * 128], ALU.mult),
                 reads=[PB[bt2]] + sbg.T, writes=fmY.T[4:8])
            yield
            out_proj_tile(s_o0, s_o1, i, fmY)

        l0_post = make_prenorm(nt) if 1 in phases else None

        def run(gen):
            try:
                next(gen)
                return True
            except StopIteration:
                return False

        for _ in stageA(0):
            pass
        s_ag = load_block(1)
        for fc in range(4):
            proj_fm(s_ag, fc * 128, N, lambda p, Tb, fc=fc: S.op(
                "act", lambda e: e.activation(sag[:, fc, 0:N], p, AF.Silu), reads=Tb, writes=[fmA.T[1]]))
        release(s_ag)
        s_bg = load_block(2)
        for fc in range(4):
            proj_fm(s_bg, fc * 128, N, lambda p, Tb, fc=fc: S.op(
                "act", lambda e: e.activation(sbg.t[:, fc, 0:N], p, AF.Silu), reads=Tb, writes=sbg.T))
        release(s_bg)
        s_o0 = load_block(7)
        s_o1 = load_block(8)
        for i in range(nt):
            gB = stageB(i)
            gA = stageA(i + 1) if i + 1 < nt else None
            run(gB)
            if gA:
                run(gA)
            run(gB)
            if l0_post is not None and i > 0:
                l0_post(i - 1)
            run(gB)
            run(gB)
            if gA:
                run(gA)
                run(gA)
                run(gA)
            else:
                release(s_q, s_k, s_v)
            run(gB)
            run(gB)
        release(s_o0, s_o1)
        if l0_post is not None:
            l0_post(nt - 1)
        S.op("pool", lambda e: e.tensor_copy(akT.t[:, 0:128], akT.t[:, N:N + 128]), reads=akT.T, writes=akT.T)

    def phase_xattn(l, nt, post=None):
        N = nt * 128
        norm_group(nt)
        qx = fmA.t
        for j in range(2):
            s = load_block((9 if l == 0 else 23) + j)
            for fc in range(4):
                proj_fm(s, fc * 128, N, lambda p, Tb, fc=fc: S.op(
                    "act", lambda e: e.copy(qx[:, 4 * j + fc, 0:N], p), reads=Tb, writes=[fmA.T[j]]))
            release(s)
        sbanks = {}

        def scores(hx):
            bs = ps2()
            sbanks[hx] = bs
            for mc in range(2):
                for dc in range(2):
                    c = 2 * hx + dc
                    S.op("pe", lambda e, mc=mc, dc=dc, c=c: e.matmul(pst[:, bs + mc, 0:N], KmT.t[:, l, c, mc * 128:(mc + 1) * 128],
                                                                     qx[:, c, 0:N], start=(dc == 0), stop=(dc == 1)),
                         reads=[KmT.T[l], fmA.T[c // 4]], writes=[PB[bs + mc]], signal=(dc == 1))
            S.op("act", lambda e: e.activation(PT2.t[:, hx % 2, :, 0:N], pst[:, bs:bs + 2, 0:N], AF.Exp, scale=1.0 / 16),
                 reads=[PB[bs], PB[bs + 1]], writes=[PT2.T[hx % 2]])

        def pv(hx):
            bn0 = ps1()
            bn1 = ps1()
            bd = ps1()
            for mc in range(2):
                S.op("pe", lambda e, mc=mc: e.matmul(pst[:, bd, 0:N], ones.t[:, 2, :], PT2.t[:, hx % 2, mc, 0:N],
                                                     start=(mc == 0), stop=(mc == 1)),
                     reads=ones.T + [PT2.T[hx % 2]], writes=[PB[bd]], signal=(mc == 1))
            for dc, bn in enumerate((bn0, bn1)):
                for mc in range(2):
                    S.op("pe", lambda e, dc=dc, mc=mc, bn=bn: e.matmul(
                        pst[:, bn, 0:N], Vm.t[:, l, mc, (2 * hx + dc) * 128:(2 * hx + dc + 1) * 128],
                        PT2.t[:, hx % 2, mc, 0:N], start=(mc == 0), stop=(mc == 1)),
                        reads=[Vm.T[l], PT2.T[hx % 2]], writes=[PB[bn]], signal=(mc == 1))
            hp = hx % 2
            S.op("dve", lambda e: e.reciprocal(rdx[:, hp, 0:N], pst[:, bd, 0:N]), reads=[PB[bd]], writes=[rdx_T[hp]])
            if hx == 3:
                for dc, bn in enumerate((bn0, bn1)):
                    S.op("dve", lambda e, dc=dc, bn=bn: e.tensor_tensor(fmY.t[:, 2 * hx + dc, 0:N], pst[:, bn, 0:N],
                                                                        rdx[:, hp, 0:N], ALU.mult),
                         reads=[PB[bn], rdx_T[hp]], writes=[fmY.T[2 * hx + dc]])
                return
            for dc, bn in enumerate((bn0, bn1)):
                S.op("act", lambda e, dc=dc, bn=bn: e.copy(numsb[:, hp, dc, 0:N], pst[:, bn, 0:N]),
                     reads=[PB[bn]], writes=[numsb_T[hp]])
            for dc in range(2):
                S.op("pool", lambda e, dc=dc: e.tensor_tensor(fmY.t[:, 2 * hx + dc, 0:N], numsb[:, hp, dc, 0:N],
                                                               rdx[:, hp, 0:N], ALU.mult),
                     reads=[numsb_T[hp], rdx_T[hp]], writes=[fmY.T[2 * hx + dc]])

        scores(0)
        for hx in range(4):
            if hx + 1 < 4:
                scores(hx + 1)
            pv(hx)
        base = 11 if l == 0 else 25
        s0 = load_block(base)
        s1 = load_block(base + 1)
        out_proj(s0, s1, nt, fmY, list(range(nt)), post)
        release(s0, s1)

    def phase_l1(nt):
        N = nt * 128
        norm_group(nt)
        for j in range(4):
            sa = load_block(13 + 2 * j)
            sb_ = load_block(14 + 2 * j)
            for f2 in range(2):
                fc = 2 * j + f2
                k = fc % 2
                Tz = [zbuf.T[fc]]
                proj_fm(sa, f2 * 128, N, lambda p, Tb: S.op(
                    "act", lambda e: e.copy(gcs.t[:, k, 0:N], p), reads=Tb, writes=[gcs.T[k]]))
                proj_fm(sa, 256 + f2 * 128, N, lambda p, Tb: S.op(
                    "dve", lambda e: e.tensor_tensor(zbuf.t[:, fc, 2:2 + N], p, gcs.t[:, k, 0:N], ALU.mult),
                    reads=Tb + [gcs.T[k]], writes=Tz))
                S.op("pool", lambda e: e.tensor_scalar(cvt.t[:, k, 0:N], zbuf.t[:, fc, 2:2 + N], cw.t[:, 2, fc:fc + 1],
                                                       cw.t[:, 3, fc:fc + 1], ALU.mult, ALU.add),
                     reads=Tz + cw.T, writes=[cvt.T[k]])
                S.op("dve", lambda e: e.scalar_tensor_tensor(cvt.t[:, k, 0:N], zbuf.t[:, fc, 1:1 + N], cw.t[:, 1, fc:fc + 1],
                                                              cvt.t[:, k, 0:N], ALU.mult, ALU.add),
                     reads=Tz + cw.T + [cvt.T[k]], writes=[cvt.T[k]])
                S.op("dve", lambda e: e.scalar_tensor_tensor(cvt.t[:, k, 0:N], zbuf.t[:, fc, 0:N], cw.t[:, 0, fc:fc + 1],
                                                              cvt.t[:, k, 0:N], ALU.mult, ALU.add),
                     reads=Tz + cw.T + [cvt.T[k]], writes=[cvt.T[k]])
                S.op("pool", lambda e: e.tensor_copy(zbuf.t[:, fc, 0:2], zbuf.t[:, fc, N:N + 2]), reads=Tz, writes=Tz)
                proj_fm(sb_, f2 * 128, N, lambda p, Tb: S.op(
                    "dve", lambda e: e.tensor_tensor(cvt.t[:, k, 0:N], p, cvt.t[:, k, 0:N], ALU.mult),
                    reads=Tb + [cvt.T[k]], writes=[cvt.T[k]]))
                proj_fm(sb_, 256 + f2 * 128, N, lambda p, Tb: S.op(
                    "act", lambda e: e.activation(sgb.t[:, k, 0:N], p, AF.Silu), reads=Tb, writes=[sgb.T[k]]))
                S.op("dve" if fc == 7 else "pool",
                     lambda e: e.tensor_tensor(fmY.t[:, fc, 0:N], cvt.t[:, k, 0:N], sgb.t[:, k, 0:N], ALU.mult),
                     reads=[cvt.T[k], sgb.T[k]], writes=[fmY.T[fc]])
            release(sa, sb_)
        s0 = load_block(21)
        s1 = load_block(22)
        out_proj(s0, s1, nt, fmY, list(range(nt)), make_prenorm(nt) if 3 in phases else None)
        release(s0, s1)

    maps = [{i: i for i in range(groups[0][1])}]
    for gi in range(1, len(groups)):
        prev = maps[gi - 1]
        free_now = [b for b in range(5) if b not in prev.values()]
        order = free_now + [prev[j] for j in sorted(prev)]
        maps.append({i: order[i] for i in range(groups[gi][1])})

    def x_load(gi, i):
        tt0 = groups[gi][0]
        ap_, t_ = XB[maps[gi][i]]
        S.dma(cq, ap_, xm.ap()[(tt0 + i) * 128:(tt0 + i + 1) * 128, :], writes=[t_])

    for i in range(groups[0][1]):
        x_load(0, i)
    for g, (t0, nt) in enumerate(groups):
        xmap.clear()
        xmap.update(maps[g])
        if g + 1 < len(groups):
            for i, b in maps[g + 1].items():
                if b not in maps[g].values():
                    x_load(g + 1, i)

        def finish_tile(i, g=g, t0=t0, nt=nt):
            gt = t0 + i
            if final:
                k = gt % 2
                c = 16 + 4 * k
                Tst = [stat.T[4 + k]]
                S.op("act", lambda e: e.activation(obuf.t[:, k], xa(i), AF.Square, accum_out=stat.t[:, c:c + 1]),
                     reads=[xT(i)], writes=[obuf.T[k]] + Tst)
                S.op("act", lambda e: e.activation(stat.t[:, c + 1:c + 2], stat.t[:, c:c + 1], AF.Ln, bias=EPS, scale=1.0 / D),
                     reads=Tst, writes=Tst)
                S.op("act", lambda e: e.activation(stat.t[:, c + 2:c + 3], stat.t[:, c + 1:c + 2], AF.Exp, scale=-0.5),
                     reads=Tst, writes=Tst)
                S.op("dve", lambda e: e.scalar_tensor_tensor(obuf.t[:, k], xa(i), stat.t[:, c + 2:c + 3], gfin.t[:],
                                                             ALU.mult, ALU.mult),
                     reads=[xT(i)] + Tst + gfin.T, writes=[obuf.T[k]])
                out_ops.append(S.dma(cq, out_d.ap()[gt * 128:(gt + 1) * 128, :], obuf.t[:, k], reads=[obuf.T[k]]))
            else:
                out_ops.append(S.dma(cq, out_d.ap()[gt * 128:(gt + 1) * 128, :], xa(i), reads=[xT(i)]))
            if g + 1 < len(groups):
                for j, b in maps[g + 1].items():
                    if b == maps[g][i]:
                        x_load(g + 1, j)
                if 0 in phases and nt == 4 and groups[g + 1][1] == 4:
                    if i >= 2:
                        early_b(i - 2)
                    if i >= 1:
                        ap_, t_ = XB[maps[g + 1][i - 1]]
                        early_a(i - 1, ap_, t_)

        last = max(phases)
        S.label = 'g%d_l0' % g
        if 0 in phases:
            phase_l0(g, t0, nt)
        S.label = 'g%d_xa0' % g
        if 1 in phases:
            phase_xattn(0, nt, finish_tile if last == 1 else make_prenorm(nt))
        S.label = 'g%d_l1' % g
        if 2 in phases:
            phase_l1(nt)
        S.label = 'g%d_xa1' % g
        if 3 in phases:
            phase_xattn(1, nt, finish_tile if last == 3 else None)
        if last in (0, 2):
            for i in range(nt):
                finish_tile(i)
    for d in out_ops:
        S._wait("sp", d)
        S._wait("pool", d)
    return nc, S


def _consts(half):
    start = 0 if half == 0 else (NT_PRE) * 128
    inv = 1.0 / (10000.0 ** np.linspace(0.0, 1.0, 64))

    def cstab(pos):
        ang = pos[:, None].astype(np.float64) * inv[None, :]
        c, s = np.cos(ang), np.sin(ang)
        return np.concatenate([c, s, -s], axis=1).astype(np.float32)

    csm = cstab(start + np.arange(NT_MAIN * 128)).reshape(NT_MAIN, 128, 192).transpose(1, 0, 2)
    csp = cstab(np.arange(NT_PRE * 128)).reshape(NT_PRE, 128, 192).transpose(1, 0, 2)
    sc = 128.0 ** -0.5
    i = np.arange(128)
    dt = np.zeros((128, 4, 128), np.float64)
    qd = np.zeros((128, 4, 128), np.float64)
    kdt = np.zeros((128, 4), np.float64)
    P = NT_PRE * 128
    kdp = np.zeros((128, NT_PRE, 4), np.float64)
    for h in range(4):
        g = GAMMA[h]
        ii, jj = i[None, :], i[:, None]
        same = (ii // 64) == (jj // 64)
        earlier = (jj // 64) < (ii // 64)
        m = np.where(same, g ** np.abs(ii - jj), np.where(earlier, g ** np.maximum(ii - jj, 0), 0.0))
        dt[:, h, :] = m * sc
        qd[:, h, :] = (g ** (i + 1.0))[None, :]
        kdt[:, h] = sc * g ** (127.0 - i)
        jg = (np.arange(NT_PRE)[None, :] * 128 + i[:, None]).astype(np.float64)
        kdp[:, :, h] = sc * g ** (P - 1.0 - jg)
    f = lambda a: np.ascontiguousarray(a.reshape(a.shape[0], -1) if a.ndim == 3 and a.shape[1] == 4 else a).astype(np.float32)
    return {
        "csm": np.ascontiguousarray(csm), "csp": np.ascontiguousarray(csp),
        "dtab": f(dt), "qdtab": f(qd), "kdtab": np.ascontiguousarray(kdt).astype(np.float32),
        "kdp": np.ascontiguousarray(kdp).astype(np.float32),
        "hbias": np.full((128, 1), -30000.0 if half == 0 else 0.0, np.float32),
        "ident": np.eye(128, dtype=np.float32),
    }


def make_in_maps(inp):
    f32 = lambda a: np.ascontiguousarray(np.asarray(a, dtype=np.float32))
    x = f32(inp["x"])
    mem = f32(inp["mem"])
    gl = [inp["e_norm"][0], inp["c_norm"][0], inp["o_norm"][0], inp["c_norm"][1],
          inp["c_mem_norm"][0], inp["c_mem_norm"][1]]
    gains = np.stack([f32(g).reshape(8, 128).T for g in gl], axis=1)
    gfin = np.ascontiguousarray(np.broadcast_to(f32(inp["final_norm"])[None, :], (128, D)))
    sk = f32(inp["e_sink"])[0]
    sink = np.zeros((128, 4), np.float32)
    for fc in range(4):
        sink[0:64, fc] = sk[fc]
        sink[64:128, fc] = sk[4 + fc]
    cwv = np.concatenate([f32(inp["o_conv_w"])[0], f32(inp["o_conv_b"])], axis=0)
    cw = np.ascontiguousarray(cwv.reshape(4, 8, 128).transpose(2, 0, 1))
    shared = {
        "e_w_in": f32(inp["e_w_in"][0]), "e_w_out": f32(inp["e_w_out"][0]),
        "o_w_in": f32(inp["o_w_in"][0]), "o_w_out": f32(inp["o_w_out"][0]),
        "c_wq0": f32(inp["c_wq"][0]), "c_wq1": f32(inp["c_wq"][1]),
        "c_wkv0": f32(inp["c_wkv"][0]), "c_wkv1": f32(inp["c_wkv"][1]),
        "c_wo0": f32(inp["c_wo"][0]), "c_wo1": f32(inp["c_wo"][1]),
        "gains": np.ascontiguousarray(gains), "gfin": gfin, "sink": sink, "cw": cw,
    }
    cons = [_consts(0), _consts(1)]
    maps = []
    for c in range(8):
        b, half = c // 2, c % 2
        m = dict(shared)
        m.update(cons[half])
        if half == 0:
            m["xm"] = np.ascontiguousarray(x[b, 0:NT_MAIN * 128])
            m["xp"] = np.zeros((NT_PRE * 128, D), np.float32)
        else:
            m["xm"] = np.ascontiguousarray(x[b, NT_PRE * 128:])
            m["xp"] = np.ascontiguousarray(x[b, 0:NT_PRE * 128])
        m["mem"] = np.ascontiguousarray(mem[b])
        maps.append(m)
    return maps


def assemble(results, B=4, SEQ=8192):
    out = np.empty((B, SEQ, D), np.float32)
    for c in range(8):
        b, half = c // 2, c % 2
        o = results[c]["out"]
        if half == 0:
            out[b, 0:4096] = o[0:4096]
        else:
            out[b, 4096:] = o[128:]
    return out


_CACHE = {}


def kernel(**inputs):
    if "nc" not in _CACHE:
        _CACHE["nc"] = build()[0]
    nc = _CACHE["nc"]
    maps = make_in_maps(inputs)
    res = run_bass_kernel_spmd(nc, maps, core_ids=list(range(8)))
    return assemble(res.results)
```
